# Optimizing a Trainium2 kernel written in Bass

```python
import jax
import jax.numpy as jnp
from jax import lax
import numpy as np

D_MODEL = 1024
BATCH = 8
SEQ = 2048
DEPTH = 2

CTX_LEN = 256
GRID_W = 64
HEAD_DIM = 64
ROPE_THETA = 10000.0
NORM_EPS = 1e-6
MOD_CHUNKS = 6

GLA_HEADS = 4
GLA_DK = 32
GLA_DV = 64
GLA_GATE_RANK = 16
GLA_GATE_TAU = 16.0
GLA_CHUNK = 64
GLA_WIDTH = GLA_HEADS * GLA_DV

SWA_Q_HEADS = 6
SWA_KV_HEADS = 2
SWA_WINDOW = 128
SWA_BLOCK = 128
SWA_WIDTH = SWA_Q_HEADS * HEAD_DIM

GQA_Q_HEADS = 6
GQA_KV_HEADS = 2
GQA_BLOCK = 128
GQA_WIDTH = GQA_Q_HEADS * HEAD_DIM

MIX_WIDTH = GLA_WIDTH + SWA_WIDTH + GQA_WIDTH

IN_SIZES = (
    GLA_HEADS * GLA_DK,
    GLA_HEADS * GLA_DK,
    GLA_WIDTH,
    GLA_WIDTH,
    2 * GLA_GATE_RANK,
    SWA_WIDTH,
    SWA_KV_HEADS * HEAD_DIM,
    SWA_KV_HEADS * HEAD_DIM,
    GQA_WIDTH,
    GQA_KV_HEADS * HEAD_DIM,
    GQA_KV_HEADS * HEAD_DIM,
)
IN_TOTAL = sum(IN_SIZES)

FFN_DIM = 2816
FFN_CONV = 3

kernel_name = "hybrid_gla_swa_gqa_prefix_dit"


def rms_norm(x, w):
    xf = x.astype(jnp.float32)
    y = xf * lax.rsqrt(jnp.mean(xf * xf, axis=-1, keepdims=True) + NORM_EPS)
    return (y * w.astype(jnp.float32)).astype(x.dtype)


def to_heads(t, n_heads):
    return t.reshape(t.shape[0], t.shape[1], n_heads, -1)


def split_cols(p):
    out, start = [], 0
    for size in IN_SIZES:
        out.append(p[..., start:start + size])
        start += size
    return out


def rope_tables_2d(n_tokens):
    rows = n_tokens // GRID_W
    row = jnp.repeat(jnp.arange(rows), GRID_W).astype(jnp.float32)
    col = (jnp.arange(rows * GRID_W) % GRID_W).astype(jnp.float32)
    n_freq = HEAD_DIM // 4
    inv_freq = ROPE_THETA ** (-jnp.arange(n_freq, dtype=jnp.float32) / n_freq)
    ang_r = row[:, None] * inv_freq[None, :]
    ang_c = col[:, None] * inv_freq[None, :]
    ang = jnp.concatenate([ang_r, ang_r, ang_c, ang_c], axis=-1)
    return jnp.cos(ang), jnp.sin(ang)


def apply_rope_2d(x, cos, sin):
    xf = x.astype(jnp.float32)
    xs = xf.reshape(xf.shape[:-1] + (2, 2, HEAD_DIM // 4))
    rot = jnp.stack([-xs[..., 1, :], xs[..., 0, :]], axis=-2).reshape(xf.shape)
    return (xf * cos[:, None, :] + rot * sin[:, None, :]).astype(x.dtype)


def gla_chunked(q, k, v, log_a, s0, with_output):
    B, T, H, DK = q.shape
    C = GLA_CHUNK
    N = T // C

    def chunks(t):
        return jnp.moveaxis(t.astype(jnp.float32).reshape(B, N, C, H, t.shape[-1]), 1, 0)

    qc, kc, vc = chunks(q), chunks(k), chunks(v)
    bc = jnp.cumsum(chunks(log_a), axis=2)
    causal = jnp.tril(jnp.ones((C, C), dtype=bool))[None, :, :, None, None]

    def step(S, inp):
        qn, kn, vn, bn = inp
        b_last = bn[:, -1]
        S_next = jnp.exp(b_last)[..., None] * S + jnp.einsum(
            'bchk,bchv->bhkv', kn * jnp.exp(b_last[:, None] - bn), vn)
        if not with_output:
            return S_next, None
        o_inter = jnp.einsum('bchk,bhkv->bchv', qn * jnp.exp(bn), S)
        decay = jnp.exp(jnp.where(causal, bn[:, :, None] - bn[:, None, :], -jnp.inf))
        att = jnp.einsum('bthk,bshk,btshk->bhts', qn, kn, decay)
        o = o_inter + jnp.einsum('bhts,bshv->bthv', att, vn)
        return S_next, o

    s_final, o = lax.scan(step, s0, (qc, kc, vc, bc))
    if not with_output:
        return None, s_final
    o = jnp.moveaxis(o, 0, 1).reshape(B, T, H, v.shape[-1])
    return o, s_final


def gla_log_decay(z, w_gate, b_gate):
    g = (z @ w_gate + b_gate).astype(jnp.float32)
    return (jax.nn.log_sigmoid(g) / GLA_GATE_TAU).reshape(z.shape[0], z.shape[1], GLA_HEADS, GLA_DK)


def gla_mixer(lat, ctx, w_gate, b_gate, out_norm, need_ctx):
    def prep(parts):
        q, k, v, r, z = parts
        return (to_heads(q, GLA_HEADS) * (GLA_DK ** -0.5), to_heads(k, GLA_HEADS),
                to_heads(v, GLA_HEADS), to_heads(r, GLA_HEADS), z)

    ql, kl, vl, rl, zl = prep(lat)
    qc, kc, vc, rc, zc = prep(ctx)
    B = ql.shape[0]
    o_lat, o_ctx = 0.0, 0.0
    for direction in range(2):
        zs = slice(direction * GLA_GATE_RANK, (direction + 1) * GLA_GATE_RANK)
        seq_l = [ql, kl, vl, gla_log_decay(zl[..., zs], w_gate[direction], b_gate[direction])]
        seq_c = [qc, kc, vc, gla_log_decay(zc[..., zs], w_gate[direction], b_gate[direction])]
        if direction == 1:
            seq_l = [jnp.flip(t, axis=1) for t in seq_l]
            seq_c = [jnp.flip(t, axis=1) for t in seq_c]
        s0 = jnp.zeros((B, GLA_HEADS, GLA_DK, GLA_DV), jnp.float32)
        oc, s_ctx = gla_chunked(*seq_c, s0, need_ctx)
        ol, _ = gla_chunked(*seq_l, s_ctx, True)
        if direction == 1:
            ol = jnp.flip(ol, axis=1)
            if need_ctx:
                oc = jnp.flip(oc, axis=1)
        o_lat = o_lat + ol
        if need_ctx:
            o_ctx = o_ctx + oc

    def finish(o, r):
        y = rms_norm(o, out_norm) * jax.nn.silu(r.astype(jnp.float32))
        return y.astype(r.dtype).reshape(r.shape[0], r.shape[1], GLA_WIDTH)

    return finish(o_lat, rl), (finish(o_ctx, rc) if need_ctx else None)


def dense_gqa(q, k, v, sink=None):
    B, Tq, Hq, Dh = q.shape
    Hkv = k.shape[2]
    G = Hq // Hkv
    qg = q.reshape(B, Tq, Hkv, G, Dh)
    s = jnp.einsum('bqhgd,bkhd->bhgqk', qg, k).astype(jnp.float32) * (Dh ** -0.5)
    if sink is not None:
        s_sink = jnp.broadcast_to(sink.astype(jnp.float32).reshape(1, Hkv, G, 1, 1), s.shape[:-1] + (1,))
        p = jax.nn.softmax(jnp.concatenate([s, s_sink], axis=-1), axis=-1)[..., :-1]
    else:
        p = jax.nn.softmax(s, axis=-1)
    out = jnp.einsum('bhgqk,bkhd->bqhgd', p.astype(q.dtype), v)
    return out.reshape(B, Tq, Hq * Dh)


def swa_latent(q, k, v, k_ctx, v_ctx, sink):
    B, T, Hq, Dh = q.shape
    Hkv = k.shape[2]
    G = Hq // Hkv
    W = SWA_BLOCK
    nb = T // W
    L = k_ctx.shape[1]
    qb = q.reshape(B, nb, W, Hkv, G, Dh)

    def band(t):
        tp = jnp.pad(t, ((0, 0), (W, W), (0, 0), (0, 0))).reshape(B, nb + 2, W, Hkv, Dh)
        return jnp.concatenate([tp[:, :-2], tp[:, 1:-1], tp[:, 2:]], axis=2)

    kb, vb = band(k), band(v)
    blk = jnp.arange(nb)[:, None]
    qpos = blk * W + jnp.arange(W)[None, :]
    kpos = (blk - 1) * W + jnp.arange(3 * W)[None, :]
    valid = ((jnp.abs(qpos[:, :, None] - kpos[:, None, :]) <= SWA_WINDOW)
             & (kpos >= 0)[:, None, :] & (kpos < T)[:, None, :])
    scale = Dh ** -0.5
    s_loc = jnp.einsum('bnqhgd,bnkhd->bnhgqk', qb, kb).astype(jnp.float32) * scale
    s_loc = jnp.where(valid[None, :, None, None], s_loc, -jnp.inf)
    s_ctx = jnp.einsum('bnqhgd,bchd->bnhgqc', qb, k_ctx).astype(jnp.float32) * scale
    s_sink = jnp.broadcast_to(sink.astype(jnp.float32).reshape(1, 1, Hkv, G, 1, 1), s_loc.shape[:-1] + (1,))
    p = jax.nn.softmax(jnp.concatenate([s_loc, s_ctx, s_sink], axis=-1), axis=-1).astype(q.dtype)
    nk = 3 * W
    out = (jnp.einsum('bnhgqk,bnkhd->bnqhgd', p[..., :nk], vb)
           + jnp.einsum('bnhgqc,bchd->bnqhgd', p[..., nk:nk + L], v_ctx))
    return out.reshape(B, T, Hq * Dh)


def gqa_latent(q, k_all, v_all):
    B, T, Hq, Dh = q.shape
    nb = T // GQA_BLOCK
    qb = jnp.moveaxis(q.reshape(B, nb, GQA_BLOCK, Hq, Dh), 1, 0)
    out = lax.map(lambda qq: dense_gqa(qq, k_all, v_all), qb)
    return jnp.moveaxis(out, 0, 1).reshape(B, T, Hq * Dh)


def token_mixers(h_lat, h_ctx, w_in, gla_w_gate, gla_b_gate, gla_out_norm, swa_sink,
                 gqa_q_norm, gqa_k_norm, w_out, cos, sin, need_ctx):
    (a_q, a_k, a_v, a_r, a_z, b_q, b_k, b_v, c_q, c_k, c_v) = split_cols(h_lat @ w_in)
    (ca_q, ca_k, ca_v, ca_r, ca_z, cb_q, cb_k, cb_v, cc_q, cc_k, cc_v) = split_cols(h_ctx @ w_in)

    a_lat, a_ctx = gla_mixer((a_q, a_k, a_v, a_r, a_z), (ca_q, ca_k, ca_v, ca_r, ca_z),
                             gla_w_gate, gla_b_gate, gla_out_norm, need_ctx)

    bq_l = apply_rope_2d(to_heads(b_q, SWA_Q_HEADS), cos, sin)
    bk_l = apply_rope_2d(to_heads(b_k, SWA_KV_HEADS), cos, sin)
    bv_l = to_heads(b_v, SWA_KV_HEADS)
    bk_c, bv_c = to_heads(cb_k, SWA_KV_HEADS), to_heads(cb_v, SWA_KV_HEADS)
    b_lat = swa_latent(bq_l, bk_l, bv_l, bk_c, bv_c, swa_sink)

    cq_l = apply_rope_2d(rms_norm(to_heads(c_q, GQA_Q_HEADS), gqa_q_norm), cos, sin)
    ck_l = apply_rope_2d(rms_norm(to_heads(c_k, GQA_KV_HEADS), gqa_k_norm), cos, sin)
    cv_l = to_heads(c_v, GQA_KV_HEADS)
    ck_c = rms_norm(to_heads(cc_k, GQA_KV_HEADS), gqa_k_norm)
    cv_c = to_heads(cc_v, GQA_KV_HEADS)
    c_lat = gqa_latent(cq_l, jnp.concatenate([ck_l, ck_c], axis=1), jnp.concatenate([cv_l, cv_c], axis=1))

    out_lat = jnp.concatenate([a_lat, b_lat, c_lat], axis=-1) @ w_out
    if not need_ctx:
        return out_lat, None
    b_ctx = dense_gqa(to_heads(cb_q, SWA_Q_HEADS), bk_c, bv_c, swa_sink)
    c_ctx_out = dense_gqa(rms_norm(to_heads(cc_q, GQA_Q_HEADS), gqa_q_norm), ck_c, cv_c)
    out_ctx = jnp.concatenate([a_ctx, b_ctx, c_ctx_out], axis=-1) @ w_out
    return out_lat, out_ctx


def conv_ffn(h, w_up, conv_w, conv_b, w_down):
    u = h @ w_up
    u = lax.conv_general_dilated(u, conv_w[:, None, :], window_strides=(1,), padding=((1, 1),),
                                 dimension_numbers=('NWC', 'WIO', 'NWC'),
                                 feature_group_count=u.shape[-1]) + conv_b
    a, g = jnp.split(u, 2, axis=-1)
    return (jax.nn.silu(a) * g) @ w_down


def setup_inputs(seed: int = 0) -> dict:
    key = jax.random.key(seed)
    ks = jax.random.split(key, 24)
    f32 = jnp.float32
    D, L, F = D_MODEL, DEPTH, FFN_DIM

    def nrm(k, shape, scale):
        return jax.random.normal(k, shape, f32) * scale

    return {
        'x': nrm(ks[0], (BATCH, SEQ, D), 1.0),
        'c': nrm(ks[1], (BATCH, D), 1.0),
        'ctx': nrm(ks[2], (BATCH, CTX_LEN, D), 1.0),
        'c_ctx': nrm(ks[3], (D,), 1.0),
        'w_mod': nrm(ks[4], (L, D, MOD_CHUNKS * D), 0.5 * D ** -0.5),
        'b_mod': nrm(ks[5], (L, MOD_CHUNKS * D), 0.01),
        'attn_pre_norm': 1.0 + nrm(ks[6], (L, D), 0.05),
        'attn_post_norm': 1.0 + nrm(ks[7], (L, D), 0.05),
        'ffn_pre_norm': 1.0 + nrm(ks[8], (L, D), 0.05),
        'ffn_post_norm': 1.0 + nrm(ks[9], (L, D), 0.05),
        'w_in': nrm(ks[10], (L, D, IN_TOTAL), D ** -0.5),
        'gla_w_gate': nrm(ks[11], (L, 2, GLA_GATE_RANK, GLA_HEADS * GLA_DK), GLA_GATE_RANK ** -0.5),
        'gla_b_gate': nrm(ks[12], (L, 2, GLA_HEADS * GLA_DK), 0.1),
        'gla_out_norm': 1.0 + nrm(ks[13], (L, GLA_DV), 0.05),
        'swa_sink': nrm(ks[14], (L, SWA_Q_HEADS), 0.5),
        'gqa_q_norm': 1.0 + nrm(ks[15], (L, HEAD_DIM), 0.05),
        'gqa_k_norm': 1.0 + nrm(ks[16], (L, HEAD_DIM), 0.05),
        'w_out': nrm(ks[17], (L, MIX_WIDTH, D), MIX_WIDTH ** -0.5),
        'ffn_w_up': nrm(ks[18], (L, D, 2 * F), D ** -0.5),
        'ffn_conv_w': nrm(ks[19], (L, FFN_CONV, 2 * F), FFN_CONV ** -0.5),
        'ffn_conv_b': nrm(ks[20], (L, 2 * F), 0.01),
        'ffn_w_down': nrm(ks[21], (L, F, D), F ** -0.5),
    }


def reference(x, c, ctx, c_ctx, w_mod, b_mod, attn_pre_norm, attn_post_norm, ffn_pre_norm,
              ffn_post_norm, w_in, gla_w_gate, gla_b_gate, gla_out_norm, swa_sink, gqa_q_norm,
              gqa_k_norm, w_out, ffn_w_up, ffn_conv_w, ffn_conv_b, ffn_w_down):
    cos, sin = rope_tables_2d(x.shape[1])
    for layer in range(DEPTH):
        need_ctx = layer < DEPTH - 1
        mod_lat = jnp.split(jax.nn.silu(c) @ w_mod[layer] + b_mod[layer], MOD_CHUNKS, axis=-1)
        mod_ctx = jnp.split(jax.nn.silu(c_ctx) @ w_mod[layer] + b_mod[layer], MOD_CHUNKS, axis=-1)
        sh1, sc1, g1, sh2, sc2, g2 = [m[:, None, :] for m in mod_lat]
        csh1, csc1, cg1, csh2, csc2, cg2 = mod_ctx

        h_lat = rms_norm(x, attn_pre_norm[layer]) * (1 + sc1) + sh1
        h_ctx = rms_norm(ctx, attn_pre_norm[layer]) * (1 + csc1) + csh1
        mix_lat, mix_ctx = token_mixers(h_lat, h_ctx, w_in[layer], gla_w_gate[layer], gla_b_gate[layer],
                                        gla_out_norm[layer], swa_sink[layer], gqa_q_norm[layer],
                                        gqa_k_norm[layer], w_out[layer], cos, sin, need_ctx)
        x = x + g1 * rms_norm(mix_lat, attn_post_norm[layer])

        h_lat = rms_norm(x, ffn_pre_norm[layer]) * (1 + sc2) + sh2
        f_lat = conv_ffn(h_lat, ffn_w_up[layer], ffn_conv_w[layer], ffn_conv_b[layer], ffn_w_down[layer])
        x = x + g2 * rms_norm(f_lat, ffn_post_norm[layer])

        if need_ctx:
            ctx = ctx + cg1 * rms_norm(mix_ctx, attn_post_norm[layer])
            h_c = rms_norm(ctx, ffn_pre_norm[layer]) * (1 + csc2) + csh2
            f_ctx = conv_ffn(h_c, ffn_w_up[layer], ffn_conv_w[layer], ffn_conv_b[layer], ffn_w_down[layer])
            ctx = ctx + cg2 * rms_norm(f_ctx, ffn_post_norm[layer])
    return x
```

```python
import contextlib
import numpy as np
import concourse.bass as bass
import concourse.mybir as mybir
from concourse.bass_utils import run_bass_kernel_spmd

F32 = mybir.dt.float32
BF16 = mybir.dt.bfloat16
AF = mybir.ActivationFunctionType
ALU = mybir.AluOpType
AX = mybir.AxisListType

D = 1024
T_LAT = 2048
T_CTX = 256
NT = 18
NLT = 16
DEPTH = 2
FF = 2816
NFC = 22
IN_TOTAL = 2080
EPS = 1e-6


class Tok:
    __slots__ = ("name", "w", "wx", "r", "dsem", "dtotal", "excl")

    def __init__(self, name, excl=False):
        self.name = name
        self.excl = excl
        self.w = None
        self.wx = []
        self.r = {}
        self.dsem = None
        self.dtotal = 0


class Op:
    __slots__ = ("eng", "fn", "deps", "inc", "val", "is_dma", "dsem", "dval", "idx")

    def __init__(self, eng, fn, is_dma=False):
        self.eng = eng
        self.fn = fn
        self.deps = []
        self.inc = False
        self.val = 0
        self.is_dma = is_dma
        self.dsem = None
        self.dval = 0


class Sched:
    ENGS = ("pe", "act", "dve", "pool", "sp")

    def __init__(self, nc, es):
        self.nc = nc
        self.es = es
        self.ops = {e: [] for e in self.ENGS}
        self.sems = {e: es.enter_context(nc.semaphore("s_" + e)) for e in ("pe", "act", "dve", "pool")}
        self.n_dsem = 0
        self.last = {e: None for e in self.ENGS}
        self.dmas = []

    def full_barrier(self):
        lasts = [o for o in self.last.values() if o is not None]
        dm = list(self.dmas)
        for e in self.ENGS:
            o = Op(e, None)
            for ev in lasts:
                if not ev.is_dma:
                    ev.inc = True
                o.deps.append(ev)
            for ev in dm:
                o.deps.append(ev)
            self.ops[e].append(o)
        self.dmas = []

    def _dep(self, op, ev):
        if ev is None or ev is op:
            return
        if (not ev.is_dma) and ev.eng == "pe" and op.eng == "pe" and not op.is_dma:
            return
        if not ev.is_dma:
            ev.inc = True
        op.deps.append(ev)

    def _track(self, op, r, w, waw=True):
        key = ("d", id(op)) if op.is_dma else op.eng
        for t in r:
            self._dep(op, t.w)
            for ev in t.wx:
                self._dep(op, ev)
            if t.excl:
                for k2, ev in t.r.items():
                    if k2 != key:
                        self._dep(op, ev)
        for t in w:
            if waw:
                self._dep(op, t.w)
                for ev in t.wx:
                    self._dep(op, ev)
            for ev in t.r.values():
                self._dep(op, ev)
        for t in r:
            t.r[key] = op
        for t in w:
            if waw:
                t.wx = []
                t.r = {}
            elif t.w is not None:
                t.wx.append(t.w)
            t.w = op

    def op(self, eng, fn, r=(), w=(), waw=True):
        o = Op(eng, fn)
        self._track(o, r, w, waw=waw)
        self.ops[eng].append(o)
        self.last[eng] = o
        return o

    def pe(self, fn, r=(), w=()):
        return self.op("pe", fn, r, w)

    def act(self, fn, r=(), w=(), waw=True):
        return self.op("act", fn, r, w, waw)

    def dve(self, fn, r=(), w=(), waw=True):
        return self.op("dve", fn, r, w, waw)

    def pool(self, fn, r=(), w=()):
        return self.op("pool", fn, r, w)

    def dma(self, q, out, in_, r=(), w=(), waw=True, semtok=None):
        o = Op(q, lambda e: e.dma_start(out=out, in_=in_), is_dma=True)
        st = semtok if semtok is not None else w[0]
        if st.dsem is None:
            st.dsem = self.es.enter_context(self.nc.semaphore("d_%d" % self.n_dsem))
            self.n_dsem += 1
        st.dtotal += 16
        o.dsem = st.dsem
        o.dval = st.dtotal
        self._track(o, r, w, waw=waw)
        self.ops[q].append(o)
        self.dmas.append(o)
        return o

    def barrier_wait(self, eng, toks):
        o = Op(eng, None)
        for t in toks:
            self._dep(o, t.w)
        self.ops[eng].append(o)
        return o

    def emit(self, block):
        for e in ("pe", "act", "dve", "pool"):
            c = 0
            for o in self.ops[e]:
                if o.is_dma:
                    continue
                if o.inc:
                    c += 1
                    o.val = c
        handles = {"pe": block.tensor, "act": block.scalar, "dve": block.vector, "pool": block.gpsimd, "sp": block.sync}

        def mk(ename):
            ops = self.ops[ename]
            sems = self.sems

            def body(e):
                known = {}
                for o in ops:
                    for ev in o.deps:
                        if ev.is_dma:
                            sem, val = ev.dsem, ev.dval
                        else:
                            sem, val = sems[ev.eng], ev.val
                        k = id(sem)
                        if known.get(k, 0) >= val:
                            continue
                        e.wait_ge(sem, val)
                        known[k] = val
                    if o.fn is None:
                        continue
                    ins = o.fn(e)
                    if o.is_dma:
                        ins.then_inc(o.dsem, 16)
                    elif o.inc:
                        ins.then_inc(sems[ename], 1)
            return body

        for ename in self.ENGS:
            if self.ops[ename]:
                handles[ename](mk(ename))


def _rope_tables():
    t = np.arange(T_LAT)
    row = (t // 64).astype(np.float32)
    col = (t % 64).astype(np.float32)
    nf = 16
    inv = (np.float32(10000.0) ** (-np.arange(nf, dtype=np.float32) / np.float32(nf))).astype(np.float32)
    ar = row[:, None] * inv[None, :]
    ac = col[:, None] * inv[None, :]
    ang = np.concatenate([ar, ar, ac, ac], axis=-1).astype(np.float32)
    cos = np.cos(ang).astype(np.float32)
    sin = np.sin(ang).astype(np.float32)
    sgn = np.concatenate([-np.ones(16), np.ones(16), -np.ones(16), np.ones(16)]).astype(np.float32)
    sin2 = sin * sgn[None, :]
    cos_t = cos.reshape(NLT, 128, 64).transpose(1, 0, 2).copy()
    sin_t = sin2.reshape(NLT, 128, 64).transpose(1, 0, 2).copy()
    return cos_t, sin_t


class Region:
    def __init__(self, arena, base, size):
        self.arena, self.base, self.size, self.off = arena, base, size, 0

    def reset(self):
        self.off = 0

    def alloc(self, shape, dt):
        esz = 4 if dt == F32 else 2
        n = 1
        for d in shape[1:]:
            n *= d
        nbytes = n * esz
        start = self.base + self.off
        self.off += (nbytes + 63) // 64 * 64
        assert self.off <= self.size, "region overflow: need %d have %d" % (self.off, self.size)
        ap = self.arena[0:shape[0], start // 2:(start + nbytes) // 2]
        if dt == F32:
            ap = ap.bitcast(F32)
        if len(shape) > 2:
            names = ["d%d" % i for i in range(len(shape) - 1)]
            pat = "p (%s) -> p %s" % (" ".join(names), " ".join(names))
            ap = ap.rearrange(pat, **{nm: sz for nm, sz in zip(names[:-1], shape[1:-1])})
        return ap


ARENA_BYTES = 123 * 1024
R0_BYTES = 36992
R1_BYTES = 9216


class Builder:
    def __init__(self, n_layers=DEPTH, debug=(), stop_after=None):
        self.n_layers = n_layers
        self.debug = set(debug)
        self.stop_after = stop_after
        self.nc = bass.Bass("TRN2", target_bir_lowering=False)
        self.es = contextlib.ExitStack()
        self.dbg_outs = {}
        self.gla_stage = False

    def din(self, name, shape, dt=F32):
        return self.nc.dram_tensor(name, list(shape), dt, kind="ExternalInput").ap()

    def dout(self, name, shape, dt=F32):
        return self.nc.dram_tensor(name, list(shape), dt, kind="ExternalOutput").ap()

    def sb(self, name, shape, dt):
        return self.es.enter_context(self.nc.sbuf_tensor("sb_" + name, list(shape), dt))[:]

    def dump(self, name, src_ap, shape, r, dt=F32):
        if name not in self.debug:
            return
        o = self.dout("dbg_" + name, shape, dt)
        tok = Tok("dbg_" + name)
        self.S.dma("sp", o, src_ap, r=r, w=[tok])
        self.out_toks.append(tok)

    def nb(self, reg, shape, dt, n=2, name="t"):
        return [(reg.alloc(shape, dt), Tok(name + str(i))) for i in range(n)]

    def phase(self, *regs):
        self.S.full_barrier()
        for r in regs:
            r.reset()

    def build(self):
        nc, es = self.nc, self.es
        with es:
            self.S = S = Sched(nc, es)
            self.out_toks = []
            self.declare_io()
            self.alloc_persistent()
            self.load_consts()
            try:
                for l in range(self.n_layers):
                    self.layer(l)
                self.store_output()
            except StopIteration:
                pass
            S.barrier_wait("sp", self.out_toks)
            es.enter_context(nc.allow_low_precision("bf16 matmul operands / intermediates by design"))
            block = es.enter_context(nc.Block())
            S.emit(block)
        return nc

    def stop(self, tag):
        if self.stop_after == tag:
            raise StopIteration

    def declare_io(self):
        L = DEPTH
        self.x_in = self.din("x", [T_LAT, D])
        self.ctx_in = self.din("ctx", [T_CTX, D])
        self.ccol_in = self.din("ccol", [128, 8, 2])
        self.w_mod = self.din("w_mod", [L, D, 6 * D])
        self.b_mod = self.din("b_mod", [L, 6 * D])
        self.normcol_in = self.din("normcol", [128, L, 2, 8])
        self.attn_post = self.din("attn_post_norm", [L, D])
        self.ffn_post = self.din("ffn_post_norm", [L, D])
        self.w_in = self.din("w_in", [L, D, IN_TOTAL])
        self.wg_aug = self.din("wg_aug", [L, 33, 256])
        self.gla_onorm = self.din("gla_out_norm", [L, 64])
        self.swa_sink = self.din("swa_sink", [L, 6])
        self.gqa_qkn = self.din("gqa_qkn", [L, 512])
        self.w_out = self.din("w_out", [L, D, D])
        self.w_up = self.din("ffn_w_up", [L, D, 2 * FF])
        self.convcol_in = self.din("convcol", [128, L, 44, 4])
        self.w_down = self.din("ffn_w_down", [L, FF, D])
        self.ident_in = self.din("c_ident", [128, 128])
        self.triu_in = self.din("c_triu", [128, 128])
        self.tril_in = self.din("c_tril", [128, 128])
        self.u16_in = self.din("c_u16", [128, 128])
        self.l16_in = self.din("c_l16", [128, 128])
        self.bm_in = self.din("c_bm", [128, 256])
        self.bmq_in = self.din("c_bmq", [128, 512])
        self.mbp_in = self.din("c_mb_prev", [128, 384])
        self.mbn_in = self.din("c_mb_next", [128, 384])
        self.cos_in = self.din("c_cos", [128, NLT, 64])
        self.sin_in = self.din("c_sin", [128, NLT, 64])
        self.y_out = self.dout("y", [T_LAT, D])
        self.wup_bf = self.nc.dram_tensor("wup_bf", [NFC // 2, 128, 8, 2, 256], BF16, kind="Internal").ap()
        self.wd_bf = self.nc.dram_tensor("wd_bf", [128, NFC, D], BF16, kind="Internal").ap()
        self.wupbf_t = [Tok("wupbf%d" % j) for j in range(NFC // 2)]
        self.wdbf_t = Tok("wdbf")

    def alloc_persistent(self):
        self.X = self.sb("X", [128, NT, D], F32)
        self.xt = [Tok("x%d" % j) for j in range(NT)]
        self.psall = self.es.enter_context(self.nc.psum_tensor("psall", [128, 8, 512], F32))[:]
        self.ps = [self.psall[:, i, :] for i in range(8)]
        self.pst = [Tok("ps%d" % i, excl=True) for i in range(8)]
        self.ident_f = self.sb("ident_f", [128, 128], F32)
        self.ident_b = self.sb("ident_b", [128, 128], BF16)
        self.cosb = self.sb("cosb", [128, NLT, 64], BF16)
        self.sinb = self.sb("sinb", [128, NLT, 64], BF16)
        self.cst = Tok("consts")
        self.cst2 = Tok("consts2")
        self.epsc = self.sb("epsc", [128, 1], F32)
        self.onec = self.sb("onec", [128, 1], F32)
        self.ccol = self.sb("ccol", [128, 8, 2], F32)
        self.normcol = self.sb("normcol", [128, DEPTH, 2, 8], F32)
        self.convcol = self.sb("convcol", [128, DEPTH, 44, 4], F32)
        self.sel = self.sb("sel", [2, 2, 128], BF16)
        self.grow = self.sb("grow", [2, 2, D], BF16)
        self.grow_t = Tok("grow")
        self.modcol = self.sb("modcol", [128, 4, 8, 2], F32)
        self.modcol_t = Tok("modcol")
        self.ABcol = self.sb("ABcol", [128, 2, 2, 8, 2], F32)
        self.ABcol_t = Tok("ABcol")
        self.ms = self.sb("ms", [128, NT], F32)
        self.rstd = self.sb("rstd", [128, NT], F32)
        self.ms_t = Tok("ms")
        self.arena = self.sb("arena", [128, ARENA_BYTES // 2], BF16)
        self.R0 = Region(self.arena, 0, R0_BYTES)
        self.R1 = Region(self.arena, R0_BYTES, R1_BYTES)
        self.R2 = Region(self.arena, R0_BYTES + R1_BYTES, ARENA_BYTES - R0_BYTES - R1_BYTES)
        self.RALL = Region(self.arena, 0, ARENA_BYTES)

    def load_consts(self):
        S = self.S
        for j in range(NLT):
            S.dma("sp", self.X[:, j, :], self.x_in[j * 128:(j + 1) * 128, :], w=[self.xt[j]])
        for j in range(2):
            S.dma("sp", self.X[:, NLT + j, :], self.ctx_in[j * 128:(j + 1) * 128, :], w=[self.xt[NLT + j]])
        c = self.cst
        S.dma("sp", self.ident_f, self.ident_in[:, :], w=[c], waw=False)
        S.dma("pool", self.ident_b, self.ident_in[:, :], w=[c], waw=False)
        S.dma("pool", self.cosb, self.cos_in[:, :, :], w=[c], waw=False)
        S.dma("pool", self.sinb, self.sin_in[:, :, :], w=[c], waw=False)
        S.dma("sp", self.ccol, self.ccol_in[:, :, :], w=[c], waw=False)
        S.dma("sp", self.normcol, self.normcol_in[:, :, :, :], w=[c], waw=False)
        S.dma("sp", self.convcol, self.convcol_in[:, :, :, :], w=[c], waw=False)
        S.dve(lambda e: e.memset(self.epsc, EPS), w=[self.cst2])
        S.dve(lambda e: e.memset(self.onec, 1.0), w=[self.cst2])
        S.dve(lambda e: e.memset(self.sel, 0.0), w=[self.cst2])
        S.dve(lambda e: e.memset(self.sel[0:1, 0, :], 1.0), w=[self.cst2])
        S.dma("pool", self.sel[1:2, 1, :], self.triu_in[0:1, :], w=[self.cst2])

    def store_output(self):
        S = self.S
        for j in range(NLT):
            t = Tok("y%d" % j)
            S.dma("sp", self.y_out[j * 128:(j + 1) * 128, :], self.X[:, j, :], r=[self.xt[j]], w=[t])
            self.out_toks.append(t)

    def layer(self, l):
        need_ctx = l < DEPTH - 1
        self.phase(self.RALL, self.R0, self.R1, self.R2)
        self.mod_vectors(l)
        self.stop("mod")
        self.phase(self.R2)
        self.HTW = 1 + T_LAT + 2 + T_CTX + 1
        self.hT = self.R0.alloc([128, 8, self.HTW], BF16)
        self.hT_t = [Tok("hT%d" % j) for j in range(NT)]
        self.hT_pad = Tok("hTpad")
        self.wing = self.R2.alloc([128, 8, 800], BF16)
        self.wing_t = Tok("wing")
        for k in range(8):
            self.S.dma("pool", self.wing[:, k, :], self.w_in[l, k * 128:(k + 1) * 128, 0:800], w=[self.wing_t], waw=False)
        self.norm_stats(list(range(NT)))
        items = [(n, 0, 128, self.hcol(n), self.hT_t[n]) for n in range(NT)]
        self.norm_tiles(l, 0, items, self.hT, self.R2)
        self.dump("hT_l%d" % l, self.hT.rearrange("p c t -> p (c t)"), [128, 8 * self.HTW], r=self.hT_t, dt=BF16)
        self.stop("norm1")
        self.gla_phase(l, need_ctx)
        self.stop("gla")
        self.phase(self.R2)
        self.kvq_phase(l, need_ctx)
        self.stop("kvq")
        self.phase(self.R0)
        self.attn_phase(l, need_ctx)
        self.stop("attn")
        self.phase(self.RALL, self.R0, self.R1, self.R2)
        self.ffn_phase(l, need_ctx)
        self.stop("ffn")

    def run_interleaved(self, gens, width):
        gens = list(gens)
        active = []
        while gens or active:
            while gens and len(active) < width:
                active.append(gens.pop(0))
            for g in list(active):
                try:
                    next(g)
                except StopIteration:
                    active.remove(g)

    def run_staged(self, gens):
        gens = list(gens)

        def run_A(g):
            for m in g:
                if m == "A_done":
                    return

        cur = gens.pop(0)
        run_A(cur)
        while cur is not None:
            nxt = gens.pop(0) if gens else None
            cur_done, nxt_done = False, nxt is None
            while not (cur_done and nxt_done):
                if not cur_done:
                    try:
                        next(cur)
                    except StopIteration:
                        cur_done = True
                if not nxt_done:
                    try:
                        if next(nxt) == "A_done":
                            nxt_done = True
                    except StopIteration:
                        nxt_done = True
            cur = nxt

    def hcol(self, n):
        return (1 + n * 128) if n < NLT else (1 + T_LAT + 2 + (n - NLT) * 128)

    def mod_vectors(self, l):
        S = self.S
        R = self.R2
        scT = R.alloc([128, 8, 2], BF16)
        sc_tmp = R.alloc([128, 16], F32)
        sc_t = Tok("scT")
        wm = self.nb(R, [128, 8, 512], F32, 3, "wm")
        wmb16 = self.nb(R, [128, 8, 512], BF16, 2, "wmb")
        bm = self.nb(R, [2, 512], F32, 2, "bm")
        modp = self.nb(R, [2, 512], F32, 2, "modp")
        ccf = self.ccol.rearrange("p c t -> p (c t)")
        S.act(lambda e: e.activation(out=sc_tmp, in_=ccf, func=AF.Exp, scale=-1.0), r=[self.cst], w=[sc_t])
        S.dve(lambda e: e.tensor_scalar_add(out=sc_tmp, in0=sc_tmp, scalar1=1.0), r=[sc_t], w=[sc_t])
        S.dve(lambda e: e.reciprocal(out=sc_tmp, in_=sc_tmp), r=[sc_t], w=[sc_t])
        S.dve(lambda e: e.tensor_tensor(out=scT.rearrange("p c t -> p (c t)"), in0=sc_tmp, in1=ccf, op=ALU.mult),
              r=[sc_t, self.cst], w=[sc_t])
        ps, pst = self.ps, self.pst
        for pc in range(12):
            sl = pc % 2
            ch, half = pc // 2, pc % 2
            wmb, wt = wm[pc % 3]
            bmb, bmt = bm[sl]
            mp, mt = modp[sl]
            S.dma("sp", wmb, self.w_mod[l, :, pc * 512:(pc + 1) * 512].rearrange("(k p) n -> p k n", p=128), w=[wt])
            S.dma("sp", bmb, self.b_mod[l, pc * 512:(pc + 1) * 512].partition_broadcast(2), w=[bmt])
            w16, w16t = wmb16[sl]
            S.act(lambda e, wmb=wmb, w16=w16: e.activation(out=w16[:, 0:4, :], in_=wmb[:, 0:4, :], func=AF.Copy), r=[wt], w=[w16t])
            S.dve(lambda e, wmb=wmb, w16=w16: e.tensor_copy(out=w16[:, 4:7, :], in_=wmb[:, 4:7, :]), r=[wt], w=[w16t], waw=False)
            S.op("pool", lambda e, wmb=wmb, w16=w16: e.tensor_copy(out=w16[:, 7:8, :], in_=wmb[:, 7:8, :]), r=[wt], w=[w16t], waw=False)
            for k in range(8):
                S.pe(lambda e, k=k, w16=w16: e.matmul(ps[0][0:2, :], scT[:, k, :], w16[:, k, :], start=(k == 0), stop=(k == 7)),
                     r=[sc_t, w16t], w=[pst[0]])
            S.dve(lambda e, mp=mp, bmb=bmb: e.tensor_tensor(out=mp, in0=ps[0][0:2, :], in1=bmb, op=ALU.add), r=[pst[0], bmt], w=[mt])
            if ch in (2, 5):
                wh = 0 if ch == 2 else 1
                S.act(lambda e, mp=mp, wh=wh, half=half: e.activation(out=self.grow[:, wh, half * 512:(half + 1) * 512], in_=mp, func=AF.Copy),
                      r=[mt], w=[self.grow_t])
            else:
                slot = {0: 0, 1: 1, 3: 2, 4: 3}[ch]
                for i in range(4):
                    S.pe(lambda e, mp=mp, i=i: e.transpose(ps[1][:, i * 2:(i + 1) * 2], mp[:, i * 128:(i + 1) * 128], self.ident_f[0:2, 0:2]),
                         r=[mt, self.cst], w=[pst[1]])
                S.dve(lambda e, slot=slot, half=half: e.tensor_copy(
                    out=self.modcol[:, slot, half * 4:(half + 1) * 4, :], in_=ps[1][:, 0:8].rearrange("p (c t) -> p c t", t=2)),
                    r=[pst[1]], w=[self.modcol_t])
        for wn in range(2):
            nw = self.normcol[:, l, wn, :].unsqueeze(2).broadcast_to([128, 8, 2])
            S.dve(lambda e, wn=wn, nw=nw: e.scalar_tensor_tensor(
                out=self.ABcol[:, wn, 0, :, :], in0=self.modcol[:, 2 * wn + 1, :, :], scalar=1.0, in1=nw, op0=ALU.add, op1=ALU.mult),
                r=[self.modcol_t, self.cst], w=[self.ABcol_t])
            S.dve(lambda e, wn=wn: e.tensor_copy(out=self.ABcol[:, wn, 1, :, :], in_=self.modcol[:, 2 * wn, :, :]),
                  r=[self.modcol_t], w=[self.ABcol_t])

    def norm_stats(self, tiles):
        S = self.S
        junk = self.R2.alloc([128, D], BF16)
        jt = Tok("sqjunk")
        for j in tiles:
            S.act(lambda e, j=j: e.activation(out=junk, in_=self.X[:, j, :], func=AF.Square, scale=1.0 / 32.0,
                                              accum_out=self.ms[:, j:j + 1]), r=[self.xt[j]], w=[jt, self.ms_t])
        S.act(lambda e: e.activation(out=self.rstd, in_=self.ms, func=AF.Ln, bias=self.epsc[:, 0:1]), r=[self.ms_t, self.cst2], w=[self.ms_t])
        S.act(lambda e: e.activation(out=self.rstd, in_=self.rstd, func=AF.Exp, scale=-0.5), r=[self.ms_t], w=[self.ms_t])

    def norm_tiles(self, l, which, items, dest, reg):
        S = self.S
        xn = self.nb(reg, [128, D], BF16, 2, "xn")
        for i, (n, lo, hi, dc, dtok) in enumerate(items):
            sl = i % 2
            lc = 0 if n < NLT else 1
            xnb, xnt = xn[sl]
            bank, bt = self.ps[2 + sl], self.pst[2 + sl]
            S.dve(lambda e, n=n, xnb=xnb: e.tensor_scalar_mul(out=xnb, in0=self.X[:, n, :], scalar1=self.rstd[:, n:n + 1]),
                  r=[self.xt[n], self.ms_t], w=[xnt])
            pb = bank.bitcast(BF16)
            for c in range(8):
                S.pe(lambda e, c=c, xnb=xnb, pb=pb: e.transpose(pb[:, c * 128:(c + 1) * 128], xnb[:, c * 128:(c + 1) * 128], self.ident_b),
                     r=[xnt, self.cst], w=[bt])
            w = hi - lo
            for c in range(8):
                if c < 4:
                    S.act(lambda e, c=c, pb=pb, lo=lo, hi=hi, dc=dc, w=w, lc=lc: e.activation(
                        out=dest[:, c, dc:dc + w], in_=pb[:, c * 128 + lo:c * 128 + hi], func=AF.Identity,
                        scale=self.ABcol[:, which, 0, c, lc:lc + 1], bias=self.ABcol[:, which, 1, c, lc:lc + 1]), r=[bt, self.ABcol_t], w=[dtok], waw=False)
                else:
                    S.dve(lambda e, c=c, pb=pb, lo=lo, hi=hi, dc=dc, w=w, lc=lc: e.tensor_scalar(
                        out=dest[:, c, dc:dc + w], in0=pb[:, c * 128 + lo:c * 128 + hi],
                        scalar1=self.ABcol[:, which, 0, c, lc:lc + 1], scalar2=self.ABcol[:, which, 1, c, lc:lc + 1],
                        op0=ALU.mult, op1=ALU.add), r=[bt, self.ABcol_t], w=[dtok], waw=False)

    def gla_phase(self, l, need_ctx):
        S = self.S
        R = self.R2
        ps, pst = self.ps, self.pst
        gc_t = Tok("gla_consts")
        triu_f = R.alloc([128, 128], F32)
        tril_f = R.alloc([128, 128], F32)
        u16 = R.alloc([128, 128], BF16)
        l16 = R.alloc([128, 128], BF16)
        bm_f = R.alloc([128, 256], F32)
        wg_sb = R.alloc([33, 256], BF16)
        onorm = R.alloc([128, 64], F32)
        for dst, src in ((triu_f, self.triu_in), (tril_f, self.tril_in), (bm_f, self.bm_in)):
            S.dma("sp", dst, src[:, :], w=[gc_t], waw=False)
        for dst, src in ((u16, self.u16_in), (l16, self.l16_in)):
            S.dma("pool", dst, src[:, :], w=[gc_t], waw=False)
        S.dma("pool", wg_sb, self.wg_aug[l, :, :], w=[gc_t], waw=False)
        S.dma("sp", onorm, self.gla_onorm[l, :].partition_broadcast(128), w=[gc_t], waw=False)
        wing, wing_t = self.wing, self.wing_t
        zsb = self.nb(R, [128, 32], BF16, 2, "zsb")
        zT = self.nb(R, [33, 128], BF16, 2, "zT")
        sp1 = self.nb(R, [128, 256], F32, 2, "sp1")
        sp1b = self.nb(R, [128, 256], BF16, 2, "sp1b")
        expB = self.nb(R, [128, 256], F32, 2, "expB")
        expNB = self.nb(R, [128, 256], F32, 2, "expNB")
        qtb = self.nb(R, [128, 2, 128], BF16, 2, "qtb")
        ktb = self.nb(R, [128, 2, 128], BF16, 2, "ktb")
        vbf = self.nb(R, [128, 256], BF16, 2, "vbf")
        kTt = self.nb(R, [128, 2, 128], BF16, 2, "kTt")
        bdq = self.nb(R, [128, 2, 512], BF16, 1, "bdq")
        attm = self.nb(R, [128, 2, 512], BF16, 1, "attm")
        ger = self.nb(R, [128, 256], F32, 2, "ger")
        qTs = R.alloc([128, NT, 2, 128], BF16)
        qTs_t = [Tok("qTs%d" % n) for n in range(NT)]
        KVS = R.alloc([128, NT, 2, 64], BF16)
        tkv = self.nb(R, [128, 256], F32, 2, "tkv")
        bdS = self.nb(R, [128, 256], BF16, 2, "bdS")
        KVS_t = [[Tok("KVS%d_%d" % (n, d)) for d in range(2)] for n in range(NT)]
        Eall = R.alloc([128, NT, 2], F32)
        E_t = [Tok("E%d" % n) for n in range(NT)]
        gate = R.alloc([128, NT, 256], BF16)
        gate_t = [Tok("gate%d" % n) for n in range(NT)]
        Rr = [self.nb(R, [128, 64], F32, 2, "R%d_" % d) for d in range(2)]
        go = self.nb(R, [128, 256], F32, 2, "go")
        gsq = self.nb(R, [128, 256], F32, 2, "gsq")
        gss = self.nb(R, [128, 4], F32, 2, "gss")
        self.oloc = self.R1.alloc([128, NT, 256], BF16)
        self.oloc_t = [Tok("oloc%d" % n) for n in range(NT)]
        oloc, oloc_t = self.oloc, self.oloc_t
        for (t, k) in zT:
            S.dve(lambda e, t=t: e.memset(t[32:33, :], 1.0), w=[k])
        bmq_f = R.alloc([128, 512], BF16)
        S.dma("pool", bmq_f, self.bmq_in[:, :], w=[gc_t], waw=False)
        qk_sb = self.nb(R, [128, 256], F32, 2, "qk_sb")

        def gla_proj(n):
            c0 = self.hcol(n)
            for (bank, lo, hi) in ((0, 0, 512), (1, 512, 800)):
                for k in range(8):
                    S.pe(lambda e, k=k, bank=bank, lo=lo, hi=hi: e.matmul(
                        ps[bank][:, 0:hi - lo], self.hT[:, k, c0:c0 + 128], wing[:, k, lo:hi], start=(k == 0), stop=(k == 7)),
                        r=[self.hT_t[n], wing_t], w=[pst[bank]])

        def tile_gen(n):
            sl = n % 2
            zs, zs_t = zsb[sl]
            zt, zt_t = zT[sl]
            S.act(lambda e: e.activation(out=zs, in_=ps[1][:, 256:288], func=AF.Copy), r=[pst[1]], w=[zs_t])
            yield
            pb2 = ps[2].bitcast(BF16)
            S.pe(lambda e: e.transpose(pb2[0:32, 0:128], zs, self.ident_b), r=[zs_t, self.cst], w=[pst[2]])
            qk, qk_t = qk_sb[sl]
            vb, vb_t = vbf[sl]
            S.act(lambda e: e.activation(out=qk, in_=ps[0][:, 0:256], func=AF.Copy), r=[pst[0]], w=[qk_t])
            S.act(lambda e: e.activation(out=vb, in_=ps[0][:, 256:512], func=AF.Copy), r=[pst[0]], w=[vb_t])
            yield
            S.dve(lambda e: e.tensor_copy(out=zt[0:32, :], in_=pb2[0:32, 0:128]), r=[pst[2]], w=[zt_t])
            yield
            S.pe(lambda e: e.matmul(ps[2][:, 128:384], zt, wg_sb, start=True, stop=True), r=[zt_t, gc_t], w=[pst[2]])
            gr, gr_t = ger[sl]
            S.act(lambda e: e.activation(out=gr, in_=ps[1][:, 0:256], func=AF.Exp, scale=-1.0), r=[pst[1]], w=[gr_t])
            yield
            s1, s1_t = sp1[sl]
            S.act(lambda e: e.activation(out=s1, in_=ps[2][:, 128:384], func=AF.Exp, scale=-1.0), r=[pst[2]], w=[s1_t])
            S.dve(lambda e: e.tensor_scalar_add(out=gr, in0=gr, scalar1=1.0), r=[gr_t], w=[gr_t])
            yield
            s1b, s1b_t = sp1b[sl]
            S.act(lambda e: e.activation(out=s1b, in_=s1, func=AF.Ln, bias=self.onec[:, 0:1]), r=[s1_t, self.cst2], w=[s1b_t])
            S.dve(lambda e: e.reciprocal(out=gr, in_=gr), r=[gr_t], w=[gr_t])
            yield
            S.pe(lambda e: e.matmul(ps[3][:, 0:128], u16, s1b[:, 0:128], start=True, stop=True), r=[s1b_t, gc_t], w=[pst[3]])
            S.pe(lambda e: e.matmul(ps[3][:, 128:256], l16, s1b[:, 128:256], start=True, stop=True), r=[s1b_t, gc_t], w=[pst[3]])
            S.pe(lambda e: e.matmul(ps[2][:, 384:386], s1b[:, 0:128], u16[:, 126:128], start=True, stop=True), r=[s1b_t, gc_t], w=[pst[2]])
            S.pe(lambda e: e.matmul(ps[2][:, 386:388], s1b[:, 128:256], u16[:, 126:128], start=True, stop=True), r=[s1b_t, gc_t], w=[pst[2]])
            S.dve(lambda e: e.tensor_tensor(out=gate[:, n, :], in0=ps[1][:, 0:256], in1=gr, op=ALU.mult), r=[pst[1], gr_t], w=[gate_t[n]])
            yield
            eB, eB_t = expB[sl]
            eN, eN_t = expNB[sl]
            S.act(lambda e: e.activation(out=eB, in_=ps[3][:, 0:256], func=AF.Exp), r=[pst[3]], w=[eB_t])
            yield
            S.act(lambda e: e.activation(out=eN, in_=ps[3][:, 0:256], func=AF.Exp, scale=-1.0), r=[pst[3]], w=[eN_t])
            S.act(lambda e: e.activation(out=Eall[:, n, :], in_=ps[2][:, 385:388:2], func=AF.Exp), r=[pst[2]], w=[E_t[n]])
            yield "A_done"
            qb, qb_t = qtb[sl]
            kb, kb_t = ktb[sl]
            S.dve(lambda e: e.scalar_tensor_tensor(
                out=qb, in0=eB.rearrange("p (d f) -> p d f", d=2), scalar=32.0 ** -0.5,
                in1=qk[:, 0:128].unsqueeze(1).broadcast_to([128, 2, 128]), op0=ALU.mult, op1=ALU.mult), r=[qk_t, eB_t], w=[qb_t])
            S.dve(lambda e: e.tensor_tensor(
                out=kb, in0=eN.rearrange("p (d f) -> p d f", d=2),
                in1=qk[:, 128:256].unsqueeze(1).broadcast_to([128, 2, 128]), op=ALU.mult), r=[qk_t, eN_t], w=[kb_t])
            yield
            pb4 = ps[4].bitcast(BF16)
            for i, (src, d) in enumerate(((qb, 0), (qb, 1), (kb, 0), (kb, 1))):
                S.pe(lambda e, i=i, src=src, d=d: e.transpose(pb4[:, i * 128:(i + 1) * 128], src[:, d, :], self.ident_b),
                     r=[qb_t if i < 2 else kb_t, self.cst], w=[pst[4]])
            yield
            kt, kt_t = kTt[sl]
            S.dve(lambda e: e.tensor_copy(out=qTs[:, n, :, :], in_=pb4[:, 0:256].rearrange("p (d t) -> p d t", d=2)), r=[pst[4]], w=[qTs_t[n]])
            S.act(lambda e: e.activation(out=kt, in_=pb4[:, 256:512].rearrange("p (d t) -> p d t", d=2), func=AF.Copy), r=[pst[4]], w=[kt_t])
            yield
            bq, bq_t = bdq[0]
            am, am_t = attm[0]
            for d in range(2):
                S.dve(lambda e, d=d: e.tensor_tensor(
                    out=bq[:, d, :].rearrange("p (h t) -> p h t", h=4),
                    in0=bmq_f.rearrange("p (h t) -> p h t", h=4),
                    in1=qTs[:, n, d, :].unsqueeze(1).broadcast_to([128, 4, 128]), op=ALU.mult), r=[qTs_t[n], gc_t], w=[bq_t])
            yield
            for d in range(2):
                S.pe(lambda e, d=d: e.matmul(ps[5 + d], kt[:, d, :], bq[:, d, :], start=True, stop=True),
                     r=[kt_t, bq_t], w=[pst[5 + d]])
            yield
            for d in range(2):
                msk = triu_f if d == 0 else tril_f
                S.dve(lambda e, d=d, msk=msk: e.tensor_tensor(
                    out=am[:, d, :].rearrange("p (h t) -> p h t", h=4), in0=ps[5 + d].rearrange("p (h t) -> p h t", h=4),
                    in1=msk.unsqueeze(1).broadcast_to([128, 4, 128]), op=ALU.mult), r=[pst[5 + d], gc_t], w=[am_t])
                yield
            for h in range(4):
                for d in range(2):
                    S.pe(lambda e, h=h, d=d: e.matmul(
                        ps[4][:, 256 + h * 64:256 + (h + 1) * 64], am[:, d, h * 128:(h + 1) * 128], vb[:, h * 64:(h + 1) * 64],
                        start=(d == 0), stop=(d == 1)), r=[am_t, vb_t], w=[pst[4]])
            for d in range(2):
                S.pe(lambda e, d=d: e.matmul(ps[7][:, d * 256:(d + 1) * 256], kb[:, d, :], vb, start=True, stop=True),
                     r=[kb_t, vb_t], w=[pst[7]])
            yield
            S.act(lambda e: e.activation(out=oloc[:, n, :], in_=ps[4][:, 256:512], func=AF.Copy), r=[pst[4]], w=[oloc_t[n]])
            for d in range(2):
                tk, tk_t = tkv[d]
                S.dve(lambda e, d=d, tk=tk: e.scalar_tensor_tensor(
                    out=tk, in0=ps[7][:, d * 256:(d + 1) * 256], scalar=Eall[:, n, d:d + 1], in1=bm_f,
                    op0=ALU.mult, op1=ALU.mult), r=[pst[7], E_t[n], gc_t], w=[tk_t])
                yield
                S.dve(lambda e, d=d, tk=tk: e.tensor_reduce(out=KVS[:, n, d, :], in_=tk.rearrange("p (h d) -> p d h", h=4), axis=AX.X, op=ALU.add),
                      r=[tk_t], w=[KVS_t[n][d]])
            yield

        for n in range(NT):
            gla_proj(n)
            for m in tile_gen(n):
                pass
        self.stop("g10")
        orders = ([16, 17] + list(range(16)), [17, 16] + list(range(15, -1, -1)))
        for d in range(2):
            S.dve(lambda e, d=d: e.memset(Rr[d][0][0], 0.0), w=[Rr[d][0][1]])
        for i in range(NT):
            for d in range(2):
                n = orders[d][i]
                Rc, Rc_t = Rr[d][i % 2]
                Rn, Rn_t = Rr[d][(i + 1) % 2]
                S.dve(lambda e, d=d, n=n, Rc=Rc, Rn=Rn: e.scalar_tensor_tensor(
                    out=Rn, in0=Rc, scalar=Eall[:, n, d:d + 1], in1=KVS[:, n, d, :], op0=ALU.mult, op1=ALU.add),
                    r=[Rc_t, E_t[n], KVS_t[n][d]], w=[Rn_t])
                S.dve(lambda e, d=d, n=n, Rc=Rc: e.tensor_copy(out=KVS[:, n, d, :], in_=Rc), r=[Rc_t], w=[KVS_t[n][d]])
        self.stop("g11")
        tiles = list(range(NT)) if need_ctx else list(range(NLT))
        bdS2 = [bdS, self.nb(R, [128, 256], BF16, 2, "bdSb")]

        def pass2_gen(n):
            sl = n % 2
            bank = sl
            for d in range(2):
                bd, bd_t = bdS2[sl][d]
                S.dve(lambda e, d=d, bd=bd: e.tensor_tensor(out=bd.rearrange("p (h d) -> p h d", h=4),
                                                            in0=bm_f.rearrange("p (h d) -> p h d", h=4),
                                                            in1=KVS[:, n, d, :].unsqueeze(1).broadcast_to([128, 4, 64]), op=ALU.mult),
                      r=[KVS_t[n][d], gc_t], w=[bd_t])
                S.pe(lambda e, d=d, bd=bd: e.matmul(ps[bank][:, 0:256], qTs[:, n, d, :], bd, start=(d == 0), stop=(d == 1)),
                     r=[qTs_t[n], bd_t], w=[pst[bank]])
                yield
            g, g_t = go[sl]
            q2, q2_t = gsq[sl]
            ss, ss_t = gss[sl]
            S.dve(lambda e: e.tensor_tensor(out=g, in0=ps[bank][:, 0:256], in1=oloc[:, n, :], op=ALU.add),
                  r=[pst[bank], oloc_t[n]], w=[g_t])
            yield
            S.act(lambda e: e.activation(out=q2, in_=g, func=AF.Square), r=[g_t], w=[q2_t])
            yield
            S.dve(lambda e: e.tensor_reduce(out=ss, in_=q2.rearrange("p (h d) -> p h d", h=4), axis=AX.X, op=ALU.add),
                  r=[q2_t], w=[ss_t])
            yield
            S.act(lambda e: e.activation(out=ss, in_=ss, func=AF.Ln, scale=1.0 / 64.0, bias=self.epsc[:, 0:1]), r=[ss_t, self.cst2], w=[ss_t])
            yield
            S.act(lambda e: e.activation(out=ss, in_=ss, func=AF.Exp, scale=-0.5), r=[ss_t], w=[ss_t])
            yield
            g3 = g.rearrange("p (h d) -> p h d", h=4)
            S.pool(lambda e: e.tensor_tensor(out=g3, in0=g3, in1=ss.unsqueeze(2).broadcast_to([128, 4, 64]), op=ALU.mult),
                   r=[g_t, ss_t], w=[g_t])
            yield
            S.pool(lambda e: e.tensor_tensor(out=g3, in0=g3, in1=onorm.unsqueeze(1).broadcast_to([128, 4, 64]), op=ALU.mult),
                   r=[g_t, gc_t], w=[g_t])
            yield
            S.pool(lambda e: e.tensor_tensor(out=oloc[:, n, :], in0=g, in1=gate[:, n, :], op=ALU.mult),
                   r=[g_t, gate_t[n]], w=[oloc_t[n]])
            yield

        self.run_interleaved([pass2_gen(n) for n in tiles], 2)
        self.dump("gla_l%d" % l, oloc.rearrange("p n f -> p (n f)"), [128, NT * 256], r=oloc_t, dt=BF16)

    def kvq_phase(self, l, need_ctx):
        S = self.S
        R = self.R2
        ps, pst = self.ps, self.pst
        wq = R.alloc([128, 8, 1280], BF16)
        wq_t = Tok("wq")
        for k in range(8):
            S.dma("pool", wq[:, k, :], self.w_in[l, k * 128:(k + 1) * 128, 800:2080], w=[wq_t], waw=False)
        self.q_tok = R.alloc([128, NT, 768], BF16)
        self.q_t = [Tok("q%d" % n) for n in range(NT)]
        self.kT = R.alloc([128, 2, NT * 128], BF16)
        self.kT_t = [Tok("kT%d" % n) for n in range(NT)]
        self.V = R.alloc([128, NT, 4, 65], BF16)
        self.V_t = [Tok("V%d" % n) for n in range(NT)]
        vones = Tok("vones")
        wqk = R.alloc([128, 512], BF16)
        wqk_t = Tok("wqk")
        S.dma("pool", wqk, self.gqa_qkn[l, :].partition_broadcast(128), w=[wqk_t])
        S.dve(lambda e: e.memset(self.V[:, :, :, 64:65], 1.0), w=[vones])
        sq = self.nb(R, [128, 512], BF16, 1, "sq") * 2
        ss8 = self.nb(R, [128, 8], F32, 2, "ss8")
        nq = self.nb(R, [128, 512], F32, 2, "nq")
        t1s = self.nb(R, [128, 512], BF16, 1, "t1s")[0]
        t2s = self.nb(R, [128, 512], BF16, 1, "t2s")[0]
        t1g = self.nb(R, [128, 512], BF16, 1, "t1g") * 2
        t2g = self.nb(R, [128, 512], BF16, 1, "t2g") * 2
        kr = self.nb(R, [128, 256], BF16, 2, "kr")

        def rope(eng, n, src, src_t, a, a_t, b, b_t, qoff, koff, krb, kr_t):
            s5 = src.rearrange("p (h r f s) -> p h r f s", h=8, r=2, f=2)
            b5 = b.rearrange("p (h r f s) -> p h r f s", h=8, r=2, f=2)
            sn5 = self.sinb[:, n, :].rearrange("p (r f s) -> p r f s", r=2, f=2)
            cosv = self.cosb[:, n, :].unsqueeze(1).broadcast_to([128, 8, 64])
            eng(lambda e: e.tensor_tensor(out=a.rearrange("p (h d) -> p h d", h=8), in0=src.rearrange("p (h d) -> p h d", h=8), in1=cosv, op=ALU.mult),
                r=[src_t, self.cst], w=[a_t])
            for f in range(2):
                eng(lambda e, f=f: e.tensor_tensor(out=b5[:, :, :, f, :], in0=s5[:, :, :, 1 - f, :],
                                                   in1=sn5[:, :, f, :].unsqueeze(1).broadcast_to([128, 8, 2, 16]), op=ALU.mult),
                    r=[src_t, self.cst], w=[b_t])
            eng(lambda e: e.tensor_tensor(out=self.q_tok[:, n, qoff:qoff + 384], in0=a[:, 0:384], in1=b[:, 0:384], op=ALU.add),
                r=[a_t, b_t], w=[self.q_t[n]])
            eng(lambda e: e.tensor_tensor(out=krb[:, koff:koff + 128], in0=a[:, 384:512], in1=b[:, 384:512], op=ALU.add),
                r=[a_t, b_t], w=[kr_t])

        def proj(n):
            sl = n % 2
            c0 = self.hcol(n)
            bA, bB, bC = 3 * sl, 3 * sl + 1, 3 * sl + 2
            for (bank, lo, hi, off) in ((bA, 0, 512, 0), (bB, 640, 1152, 0), (bC, 512, 640, 0), (bC, 1152, 1280, 128)):
                for k in range(8):
                    S.pe(lambda e, k=k, bank=bank, lo=lo, hi=hi, off=off: e.matmul(
                        ps[bank][:, off:off + hi - lo], self.hT[:, k, c0:c0 + 128], wq[:, k, lo:hi], start=(k == 0), stop=(k == 7)),
                        r=[self.hT_t[n], wq_t], w=[pst[bank]])

        def tile(n):
            sl = n % 2
            lat = n < NLT
            bA, bB, bC = 3 * sl, 3 * sl + 1, 3 * sl + 2
            krb, kr_t = kr[sl]
            sqb, sq_t = sq[sl]
            s8, s8_t = ss8[sl]
            nqb, nq_t = nq[sl]
            S.act(lambda e: e.activation(out=self.V[:, n, :, 0:64], in_=ps[bC][:, 0:256].rearrange("p (g d) -> p g d", g=4), func=AF.Copy),
                  r=[pst[bC], vones], w=[self.V_t[n]])
            S.act(lambda e: e.activation(out=sqb, in_=ps[bB], func=AF.Square), r=[pst[bB]], w=[sq_t])
            if lat:
                rope(S.dve, n, ps[bA], pst[bA], t1s[0], t1s[1], t2s[0], t2s[1], 0, 0, krb, kr_t)
            else:
                S.act(lambda e: e.activation(out=self.q_tok[:, n, 0:384], in_=ps[bA][:, 0:384], func=AF.Copy), r=[pst[bA]], w=[self.q_t[n]])
                S.act(lambda e: e.activation(out=krb[:, 0:128], in_=ps[bA][:, 384:512], func=AF.Copy), r=[pst[bA]], w=[kr_t])
            S.dve(lambda e: e.tensor_reduce(out=s8, in_=sqb.rearrange("p (h d) -> p h d", h=8), axis=AX.X, op=ALU.add), r=[sq_t], w=[s8_t])
            S.act(lambda e: e.activation(out=s8, in_=s8, func=AF.Ln, scale=1.0 / 64.0, bias=self.epsc[:, 0:1]), r=[s8_t, self.cst2], w=[s8_t])
            S.act(lambda e: e.activation(out=s8, in_=s8, func=AF.Exp, scale=-0.5), r=[s8_t], w=[s8_t])
            nq3 = nqb.rearrange("p (h d) -> p h d", h=8)
            S.dve(lambda e: e.tensor_tensor(out=nq3, in0=ps[bB].rearrange("p (h d) -> p h d", h=8),
                                            in1=s8.unsqueeze(2).broadcast_to([128, 8, 64]), op=ALU.mult), r=[pst[bB], s8_t], w=[nq_t])
            S.dve(lambda e: e.tensor_tensor(out=nqb, in0=nqb, in1=wqk, op=ALU.mult), r=[nq_t, wqk_t], w=[nq_t])
            if lat:
                (a, a_t), (b, b_t) = t1g[sl], t2g[sl]
                rope(S.pool, n, nqb, nq_t, a, a_t, b, b_t, 384, 128, krb, kr_t)
            else:
                S.act(lambda e: e.activation(out=self.q_tok[:, n, 384:768], in_=nqb[:, 0:384], func=AF.Copy), r=[nq_t], w=[self.q_t[n]])
                S.act(lambda e: e.activation(out=krb[:, 128:256], in_=nqb[:, 384:512], func=AF.Copy), r=[nq_t], w=[kr_t])
            pb6 = ps[6 + sl].bitcast(BF16)
            for m in range(2):
                S.pe(lambda e, m=m: e.transpose(pb6[:, m * 128:(m + 1) * 128], krb[:, m * 128:(m + 1) * 128], self.ident_b),
                     r=[kr_t, self.cst], w=[pst[6 + sl]])
            S.act(lambda e: e.activation(out=self.kT[:, :, n * 128:(n + 1) * 128], in_=pb6[:, 0:256].rearrange("p (g t) -> p g t", g=2), func=AF.Copy),
                  r=[pst[6 + sl]], w=[self.kT_t[n]])

        proj(0)
        for n in range(NT):
            if n + 1 < NT:
                proj(n + 1)
            tile(n)
        self.dump("q_l%d" % l, self.q_tok.rearrange("p n f -> p (n f)"), [128, NT * 768], r=self.q_t, dt=BF16)
        self.dump("kT_l%d" % l, self.kT.rearrange("p g t -> p (g t)"), [128, 2 * NT * 128], r=self.kT_t, dt=BF16)
        self.dump("V_l%d" % l, self.V.rearrange("p n g d -> p (n g d)"), [128, NT * 4 * 65], r=self.V_t, dt=BF16)

    def gbc_build(self, l, wh, postw_dram, reg, pw, pw_t):
        S = self.S
        ps, pst = self.ps, self.pst
        Gbc = reg.alloc([128, 2, D], BF16)
        G_t = Tok("Gbc")
        S.dma("sp", pw, postw_dram[l, :].partition_broadcast(128), w=[pw_t])
        for lc in range(2):
            for half in range(2):
                bank = 2 * lc + half
                S.pe(lambda e, lc=lc, half=half, bank=bank: e.matmul(ps[bank], self.sel[:, lc, :], self.grow[:, wh, half * 512:(half + 1) * 512],
                                                                     start=True, stop=True), r=[self.grow_t, self.cst2], w=[pst[bank]])
                S.dve(lambda e, lc=lc, half=half, bank=bank: e.tensor_tensor(out=Gbc[:, lc, half * 512:(half + 1) * 512], in0=ps[bank],
                                                                             in1=pw[:, half * 512:(half + 1) * 512], op=ALU.mult),
                      r=[pst[bank], pw_t], w=[G_t])
        return Gbc, G_t

    def residual_epilogue(self, n, banks, Gbc, G_t, tmp, tmp_t, st, st_t):
        S = self.S
        ps, pst = self.ps, self.pst
        lc = 0 if n < NLT else 1
        for half, bank in enumerate(banks):
            S.act(lambda e, half=half, bank=bank: e.activation(out=tmp[:, half * 512:(half + 1) * 512], in_=ps[bank], func=AF.Square, scale=1.0 / 32.0,
                                                              accum_out=st[:, half:half + 1]), r=[pst[bank]], w=[tmp_t, st_t])
        S.dve(lambda e: e.tensor_tensor(out=st[:, 2:3], in0=st[:, 0:1], in1=st[:, 1:2], op=ALU.add), r=[st_t], w=[st_t])
        S.act(lambda e: e.activation(out=st[:, 3:4], in_=st[:, 2:3], func=AF.Ln, bias=self.epsc[:, 0:1]), r=[st_t, self.cst2], w=[st_t])
        S.act(lambda e: e.activation(out=st[:, 3:4], in_=st[:, 3:4], func=AF.Exp, scale=-0.5), r=[st_t], w=[st_t])
        for half, bank in enumerate(banks):
            S.dve(lambda e, half=half, bank=bank: e.scalar_tensor_tensor(
                out=tmp[:, half * 512:(half + 1) * 512], in0=ps[bank], scalar=st[:, 3:4], in1=Gbc[:, lc, half * 512:(half + 1) * 512],
                op0=ALU.mult, op1=ALU.mult), r=[pst[bank], st_t, G_t], w=[tmp_t])
        S.dve(lambda e: e.tensor_tensor(out=self.X[:, n, :], in0=self.X[:, n, :], in1=tmp, op=ALU.add), r=[tmp_t, self.xt[n]], w=[self.xt[n]])

    def attn_phase(self, l, need_ctx):
        S = self.S
        R = self.R0
        ps, pst = self.ps, self.pst
        wo = R.alloc([128, 8, D], BF16)
        wo_t = Tok("wo")
        for k in range(8):
            S.dma("pool", wo[:, k, :], self.w_out[l, k * 128:(k + 1) * 128, :], w=[wo_t], waw=False)
        mbp = R.alloc([128, 384], BF16)
        mbn = R.alloc([128, 384], BF16)
        mb_t = Tok("mb")
        S.dma("pool", mbp, self.mbp_in[:, :], w=[mb_t], waw=False)
        S.dma("pool", mbn, self.mbn_in[:, :], w=[mb_t], waw=False)
        esink = R.alloc([128, 6], F32)
        es_t = Tok("esink")
        S.dma("sp", esink, self.swa_sink[l, :].partition_broadcast(128), w=[es_t])
        S.act(lambda e: e.activation(out=esink, in_=esink, func=AF.Exp), r=[es_t], w=[es_t])
        tmp, tmp_t = self.nb(R, [128, D], F32, 1, "etmp")[0]
        Gbc, G_t = self.gbc_build(l, 0, self.attn_post, R, tmp, tmp_t)
        for j in range(NFC // 2):
            for half in range(2):
                src = self.w_up[l, :, half * FF + j * 256: half * FF + (j + 1) * 256].rearrange("(k p) f -> p k f", p=128)
                S.dma("pool", self.wup_bf[j, :, :, half, :], src, w=[self.wupbf_t[j]], waw=(half == 0))
        for c in range(NFC):
            S.dma("pool", self.wd_bf[:, c, :], self.w_down[l, c * 128:(c + 1) * 128, :], w=[self.wdbf_t], waw=(c == 0))
        Ra = Region(self.arena, self.R2.base, 20480)
        qT = self.nb(Ra, [128, 6, 128], BF16, 2, "qT")
        PT = self.nb(Ra, [128, 2, 384], BF16, 4, "PT")
        mix = self.nb(Ra, [128, 768], BF16, 2, "mix")
        mixT = self.nb(Ra, [128, 8, 128], BF16, 2, "mixT")
        den = self.nb(Ra, [128, 12], F32, 2, "den")
        stt = self.nb(Ra, [128, 4], F32, 2, "est")
        tiles = list(range(NT)) if need_ctx else list(range(NLT))
        LA = 2
        SB = (2, 4)
        ctr = {"p": 0, "s": 0}
        fifo = []

        def drain(limit):
            while fifo and (sum(1 for k, _ in fifo if k == "pv") > limit or fifo[0][0] == "tail"):
                kind, fn = fifo.pop(0)
                fn()

        for qi, n in enumerate(tiles):
            sl = qi % 2
            lat = n < NLT
            qTb, qT_t = qT[sl]
            dn, dn_t = den[sl]
            mx, mx_t = mix[sl]
            for m in range(2):
                pb = ps[m].bitcast(BF16)
                for j in range(3):
                    S.pe(lambda e, m=m, j=j, n=n, pb=pb: e.transpose(pb[:, j * 128:(j + 1) * 128],
                                                                     self.q_tok[:, n, m * 384 + (2 + j) * 64: m * 384 + (4 + j) * 64], self.ident_b),
                         r=[self.q_t[n], self.cst], w=[pst[m]])
                for j in range(3):
                    S.pe(lambda e, m=m, j=j, n=n, pb=pb: e.transpose(pb[0:64, j * 128:(j + 1) * 128],
                                                                     self.q_tok[:, n, m * 384 + j * 64: m * 384 + (j + 1) * 64], self.ident_b),
                         r=[self.q_t[n], self.cst], w=[pst[m]])
                S.act(lambda e, m=m, qTb=qTb, pb=pb: e.activation(out=qTb[:, m * 3:(m + 1) * 3, :],
                                                                  in_=pb[:, 0:384].rearrange("p (h t) -> p h t", h=3), func=AF.Copy),
                      r=[pst[m]], w=[qT_t])
            for mixer in range(2):
                obank = 6 + mixer
                if mixer == 0:
                    if lat:
                        kts = ([(n - 1, mbp)] if n >= 1 else []) + [(n, None)] + ([(n + 1, mbn)] if n <= NLT - 2 else []) + [(16, None), (17, None)]
                    else:
                        kts = [(16, None), (17, None)]
                else:
                    kts = [(k, None) for k in range(NT)] if lat else [(16, None), (17, None)]
                nk = len(kts)
                for ki, (kt, mb) in enumerate(kts):
                    sb0 = SB[ctr["s"] % 2]
                    ctr["s"] += 1
                    P, P_t = PT[ctr["p"] % len(PT)]
                    ctr["p"] += 1
                    for g in range(2):
                        S.pe(lambda e, kt=kt, qTb=qTb, mixer=mixer, g=g, sb0=sb0, mb=mb: e.matmul(
                            ps[sb0 + g][:, 0:384], self.kT[g * 64:(g + 1) * 64, mixer, kt * 128:(kt + 1) * 128],
                            qTb[g * 64:(g + 1) * 64, mixer * 3:(mixer + 1) * 3, :], start=True, stop=(mb is None)),
                            r=[self.kT_t[kt], qT_t], w=[pst[sb0 + g]])
                    if mb is not None:
                        for g in range(2):
                            S.pe(lambda e, sb0=sb0, g=g, mb=mb: e.matmul(ps[sb0 + g][:, 0:384], self.ident_b, mb, start=False, stop=True),
                                 r=[self.cst, mb_t], w=[pst[sb0 + g]])
                    S.act(lambda e, P=P, sb0=sb0: e.activation(out=P, in_=self.psall[:, sb0:sb0 + 2, 0:384], func=AF.Exp, scale=0.125),
                          r=[pst[sb0], pst[sb0 + 1]], w=[P_t])
                    last = (ki == nk - 1)

                    def pv(P=P, P_t=P_t, kt=kt, ki=ki, obank=obank, nk=nk, last=last, mixer=mixer, dn=dn, dn_t=dn_t, mx=mx, mx_t=mx_t):
                        for g in range(2):
                            kvi = 2 * mixer + g
                            for h in range(3):
                                hh = 3 * g + h
                                S.pe(lambda e, h=h, hh=hh, g=g, kvi=kvi: e.matmul(
                                    ps[obank][:, hh * 65:(hh + 1) * 65], P[:, g, h * 128:(h + 1) * 128], self.V[:, kt, kvi, :],
                                    start=(ki == 0 and g == 0 and h == 0), stop=(ki == nk - 1)), r=[P_t, self.V_t[kt]], w=[pst[obank]])
                        if not last:
                            return
                        o6 = ps[obank][:, 0:390].rearrange("p (h d) -> p h d", h=6)
                        dsl = dn[:, mixer * 6:(mixer + 1) * 6]
                        if mixer == 0:
                            S.dve(lambda e: e.tensor_tensor(out=dsl.unsqueeze(2), in0=o6[:, :, 64:65], in1=esink.unsqueeze(2), op=ALU.add),
                                  r=[pst[obank], es_t], w=[dn_t])
                        else:
                            S.dve(lambda e: e.tensor_copy(out=dsl.unsqueeze(2), in_=o6[:, :, 64:65]), r=[pst[obank]], w=[dn_t])
                        S.dve(lambda e: e.reciprocal(out=dsl, in_=dsl), r=[dn_t], w=[dn_t])
                        S.dve(lambda e: e.tensor_tensor(
                            out=mx[:, mixer * 384:(mixer + 1) * 384].rearrange("p (h d) -> p h d", h=6), in0=o6[:, :, 0:64],
                            in1=dsl.unsqueeze(2).broadcast_to([128, 6, 64]), op=ALU.mult), r=[pst[obank], dn_t], w=[mx_t])

                    fifo.append(("pv", pv))
                    drain(LA)

            def tail(n=n, sl=sl, mx=mx, mx_t=mx_t):
                self.dump("mix_l%d_%d" % (l, n), mx, [128, 768], r=[mx_t], dt=BF16)
                mT, mT_t = mixT[sl]
                pb = ps[0].bitcast(BF16)
                for c in range(8):
                    src = self.oloc[:, n, c * 128:(c + 1) * 128] if c < 2 else mx[:, (c - 2) * 128:(c - 1) * 128]
                    S.pe(lambda e, c=c, src=src: e.transpose(pb[:, c * 128:(c + 1) * 128], src, self.ident_b),
                         r=[self.oloc_t[n], mx_t, self.cst], w=[pst[0]])
                S.act(lambda e: e.activation(out=mT, in_=pb.rearrange("p (c t) -> p c t", c=8), func=AF.Copy), r=[pst[0]], w=[mT_t])
                for half in range(2):
                    for c in range(8):
                        S.pe(lambda e, c=c, half=half: e.matmul(ps[half], mT[:, c, :], wo[:, c, half * 512:(half + 1) * 512],
                                                                start=(c == 0), stop=(c == 7)), r=[mT_t, wo_t], w=[pst[half]])
                st, st_t = stt[sl]
                self.residual_epilogue(n, (0, 1), Gbc, G_t, tmp, tmp_t, st, st_t)
                S.act(lambda e: e.activation(out=tmp, in_=self.X[:, n, :], func=AF.Square, scale=1.0 / 32.0,
                                             accum_out=self.ms[:, n:n + 1]), r=[self.xt[n]], w=[tmp_t, self.ms_t], waw=False)

            fifo.append(("tail", tail))
        drain(-1)
        while fifo:
            fifo.pop(0)[1]()
        if "xattn_l%d" % l in self.debug:
            self.dump("xattn_l%d" % l, self.X.rearrange("p n f -> p (n f)"), [128, NT * D], r=self.xt)

    def ffn_phase(self, l, need_ctx):
        S = self.S
        R = self.RALL
        ps, pst = self.ps, self.pst
        wd = R.alloc([128, NFC, D], BF16)
        wd_t = Tok("wd")
        S.dma("sp", wd, self.wd_bf, r=[self.wdbf_t], w=[wd_t])
        tmp, tmp_t = self.nb(R, [128, D], F32, 1, "ftmp")[0]
        Gbc, G_t = self.gbc_build(l, 1, self.ffn_post, R, tmp, tmp_t)
        WMAX = 4 * 128 + 2
        hTbs = self.nb(R, [128, 8, WMAX], BF16, 2, "hTb")
        actT = R.alloc([128, NFC, WMAX], BF16)
        actT_t = Tok("actT")
        halo, halo_t = self.nb(R, [128, 8, 1], BF16, 1, "halo")[0]
        wu = self.nb(R, [128, 8, 2, 256], BF16, 2, "wu")
        acc = [self.nb(R, [128, 2, 256], F32, 2, "acc%d" % h) for h in range(2)]
        sil = self.nb(R, [128, 2, 256], BF16, 2, "sil")
        stt = self.nb(R, [128, 4], F32, 2, "fst")
        tiles_all = list(range(NT)) if need_ctx else list(range(NLT))
        xn = self.nb(R, [128, D], BF16, 2, "xn")
        S.act(lambda e: e.activation(out=self.rstd, in_=self.ms, func=AF.Ln, bias=self.epsc[:, 0:1]), r=[self.ms_t, self.cst2], w=[self.ms_t])
        S.act(lambda e: e.activation(out=self.rstd, in_=self.rstd, func=AF.Exp, scale=-0.5), r=[self.ms_t], w=[self.ms_t])
        self._ffn_xn = xn
        self._ffn_hv1 = self.nb(R, [128, 8, 1], F32, 1, "hv1")[0]
        blocks = [[0, 1, 2, 3], [4, 5, 6, 7], [8, 9, 10, 11], [12, 13, 14, 15]] + ([[16, 17]] if need_ctx else [])
        wu_i = 0
        ep_i = 0

        def block_norm(bi):
            blk = blocks[bi]
            hTb, hTb_t = hTbs[bi % 2]
            W = len(blk) * 128 + 2
            items = []
            first, last = blk[0], blk[-1]
            zero_cols = []
            if first in (0, 16):
                zero_cols.append(0)
            else:
                S.dve(lambda e: e.tensor_copy(out=hTb[:, :, 0:1], in_=halo), r=[halo_t], w=[hTb_t])
            for i, n in enumerate(blk):
                items.append((n, 0, 128, 1 + i * 128))
            if last in (15, 17):
                zero_cols.append(W - 1)
            else:
                items.append((last + 1, 0, 1, W - 1))
            for zc in zero_cols:
                S.dve(lambda e, zc=zc: e.memset(hTb[:, :, zc:zc + 1], 0.0), w=[hTb_t])
            self.norm_tiles_ffn(l, items, hTb, hTb_t)
            S.dve(lambda e, W=W: e.tensor_copy(out=halo, in_=hTb[:, :, W - 2:W - 1]), r=[hTb_t], w=[halo_t])

        block_norm(0)
        self.stop("f1")
        for bi, blk in enumerate(blocks):
            hTb, hTb_t = hTbs[bi % 2]
            W = len(blk) * 128 + 2
            Lg = (W - 2) // 2
            groups = ((0, Lg + 2), (Lg, W))
            for j in range(NFC // 2):
                wub, wu_t = wu[wu_i % 2]
                wu_i += 1
                S.dma("sp", wub, self.wup_bf[j], r=[self.wupbf_t[j]], w=[wu_t])
                for pi in range(2):
                    i = 2 * j + pi
                    par = i % 2
                    for half in range(2):
                        for gi, (c_lo, c_hi) in enumerate(groups):
                            bank = 4 * par + 2 * half + gi
                            for k in range(8):
                                S.pe(lambda e, k=k, bank=bank, wub=wub, half=half, pi=pi, c_lo=c_lo, c_hi=c_hi, hTb=hTb: e.matmul(
                                    ps[bank][:, 0:c_hi - c_lo], wub[:, k, half, pi * 128:(pi + 1) * 128], hTb[:, k, c_lo:c_hi],
                                    start=(k == 0), stop=(k == 7)), r=[wu_t, hTb_t], w=[pst[bank]])
                    self._ffn_pair_post(l, i, par, Lg, acc, sil, actT, actT_t)
                    continue
                    (aa, aa_t), (ag, ag_t) = acc[0][par], acc[1][par]
                    sb_, sb_t = sil[par]
                    pv = [self.psall[:, 4 * par + 2 * half:4 * par + 2 * half + 2, :] for half in range(2)]
                    ptk = [[pst[4 * par + 2 * half], pst[4 * par + 2 * half + 1]] for half in range(2)]
                    ccs = [self.convcol[:, l, half * NFC + i, :] for half in range(2)]
                    for half, (a, a_t) in enumerate(((aa, aa_t), (ag, ag_t))):
                        S.act(lambda e, a=a, half=half: e.activation(out=a[:, :, 0:Lg], in_=pv[half][:, :, 1:1 + Lg], func=AF.Identity,
                                                                    scale=ccs[half][:, 1:2], bias=ccs[half][:, 3:4]), r=ptk[half] + [self.cst], w=[a_t])
                    for half, (a, a_t) in enumerate(((aa, aa_t), (ag, ag_t))):
                        for (off, wi) in ((0, 0), (2, 2)):
                            S.dve(lambda e, a=a, half=half, off=off, wi=wi: e.scalar_tensor_tensor(
                                out=a[:, :, 0:Lg], in0=pv[half][:, :, off:off + Lg], scalar=ccs[half][:, wi:wi + 1], in1=a[:, :, 0:Lg],
                                op0=ALU.mult, op1=ALU.add), r=ptk[half] + [self.cst, a_t], w=[a_t])
                        if half == 0:
                            S.act(lambda e: e.activation(out=sb_[:, :, 0:Lg], in_=aa[:, :, 0:Lg], func=AF.Silu), r=[aa_t], w=[sb_t])
                    S.dve(lambda e, i=i: e.tensor_tensor(out=actT[:, i, 1:1 + 2 * Lg].rearrange("p (g t) -> p g t", g=2),
                                                         in0=sb_[:, :, 0:Lg], in1=ag[:, :, 0:Lg], op=ALU.mult),
                          r=[sb_t, ag_t], w=[actT_t])
                    if bi == 0 and i == 0: self.stop("f2")
            if bi == 0: self.stop("f3")
            if bi + 1 < len(blocks):
                block_norm(bi + 1)
            if bi == 0: self.stop("f4")
            for i, n in enumerate(blk):
                banks = (2 * (ep_i % 2), 2 * (ep_i % 2) + 1)
                st, st_t = stt[ep_i % 2]
                ep_i += 1
                for half in range(2):
                    for c in range(NFC):
                        S.pe(lambda e, c=c, half=half, i=i, bank=banks[half]: e.matmul(
                            self.ps[bank], actT[:, c, 1 + i * 128:1 + (i + 1) * 128], wd[:, c, half * 512:(half + 1) * 512],
                            start=(c == 0), stop=(c == NFC - 1)), r=[actT_t, wd_t], w=[pst[banks[half]]])
                self.residual_epilogue(n, banks, Gbc, G_t, tmp, tmp_t, st, st_t)
        if "xout_l%d" % l in self.debug:
            self.dump("xout_l%d" % l, self.X.rearrange("p n f -> p (n f)"), [128, NT * D], r=self.xt)

    def _ffn_pair_post(self, l, i, par, Lg, acc, sil, actT, actT_t):
        S = self.S
        pst = self.pst
        (aa, aa_t), (ag, ag_t) = acc[0][par], acc[1][par]
        sb_, sb_t = sil[par]
        pv = [self.psall[:, 4 * par + 2 * half:4 * par + 2 * half + 2, :] for half in range(2)]
        ptk = [[pst[4 * par + 2 * half], pst[4 * par + 2 * half + 1]] for half in range(2)]
        ccs = [self.convcol[:, l, half * NFC + i, :] for half in range(2)]
        for half, (a, a_t) in enumerate(((aa, aa_t), (ag, ag_t))):
            S.act(lambda e, a=a, half=half: e.activation(out=a[:, :, 0:Lg], in_=pv[half][:, :, 1:1 + Lg], func=AF.Identity,
                                                        scale=ccs[half][:, 1:2], bias=ccs[half][:, 3:4]), r=ptk[half] + [self.cst], w=[a_t])
        for half, (a, a_t) in enumerate(((aa, aa_t), (ag, ag_t))):
            for (off, wi) in ((0, 0), (2, 2)):
                S.dve(lambda e, a=a, half=half, off=off, wi=wi: e.scalar_tensor_tensor(
                    out=a[:, :, 0:Lg], in0=pv[half][:, :, off:off + Lg], scalar=ccs[half][:, wi:wi + 1], in1=a[:, :, 0:Lg],
                    op0=ALU.mult, op1=ALU.add), r=ptk[half] + [self.cst, a_t], w=[a_t])
            if half == 0:
                S.act(lambda e: e.activation(out=sb_[:, :, 0:Lg], in_=aa[:, :, 0:Lg], func=AF.Silu), r=[aa_t], w=[sb_t])
        S.dve(lambda e: e.tensor_tensor(out=actT[:, i, 1:1 + 2 * Lg].rearrange("p (g t) -> p g t", g=2),
                                        in0=sb_[:, :, 0:Lg], in1=ag[:, :, 0:Lg], op=ALU.mult),
              r=[sb_t, ag_t], w=[actT_t])

    def norm_tiles_ffn(self, l, items, dest, dtok):
        S = self.S
        xn = self._ffn_xn
        for i, (n, lo, hi, dc) in enumerate(items):
            sl = i % 2
            lc = 0 if n < NLT else 1
            xnb, xnt = xn[sl]
            bank, bt = self.ps[2 + sl], self.pst[2 + sl]
            S.dve(lambda e, n=n, xnb=xnb: e.tensor_scalar_mul(out=xnb, in0=self.X[:, n, :], scalar1=self.rstd[:, n:n + 1]),
                  r=[self.xt[n], self.ms_t], w=[xnt])
            pb = bank.bitcast(BF16)
            for c in range(8):
                S.pe(lambda e, c=c, xnb=xnb, pb=pb: e.transpose(pb[:, c * 128:(c + 1) * 128], xnb[:, c * 128:(c + 1) * 128], self.ident_b),
                     r=[xnt, self.cst], w=[bt])
            w = hi - lo
            if w == 1:
                hv, hvt = self._ffn_hv1
                Ab = self.ABcol[:, 1, 0, :, lc:lc + 1]
                Bb = self.ABcol[:, 1, 1, :, lc:lc + 1]
                pv = pb.rearrange("p (c t) -> p c t", c=8)[:, :, lo:hi]
                S.dve(lambda e, pv=pv, hv=hv, Ab=Ab: e.tensor_tensor(out=hv, in0=pv, in1=Ab, op=ALU.mult), r=[bt, self.ABcol_t], w=[hvt])
                S.dve(lambda e, hv=hv, Bb=Bb, dc=dc: e.tensor_tensor(out=dest[:, :, dc:dc + 1], in0=hv, in1=Bb, op=ALU.add),
                      r=[hvt, self.ABcol_t], w=[dtok])
                continue
            for c in range(8):
                S.act(lambda e, c=c, pb=pb, lo=lo, hi=hi, dc=dc, w=w, lc=lc: e.activation(
                    out=dest[:, c, dc:dc + w], in_=pb[:, c * 128 + lo:c * 128 + hi], func=AF.Identity,
                    scale=self.ABcol[:, 1, 0, c, lc:lc + 1], bias=self.ABcol[:, 1, 1, c, lc:lc + 1]), r=[bt, self.ABcol_t], w=[dtok], waw=False)


def make_in_maps(inputs, n_cores=8):
    f = lambda a: np.ascontiguousarray(np.asarray(a, dtype=np.float32))
    x, c, ctx, c_ctx = f(inputs["x"]), f(inputs["c"]), f(inputs["ctx"]), f(inputs["c_ctx"])
    L = DEPTH
    normcol = np.stack([f(inputs["attn_pre_norm"]), f(inputs["ffn_pre_norm"])], axis=1)
    normcol = normcol.reshape(L, 2, 8, 128).transpose(3, 0, 1, 2).copy()
    cw = f(inputs["ffn_conv_w"])
    cb = f(inputs["ffn_conv_b"])
    conv = np.concatenate([cw, cb[:, None, :]], axis=1)
    convcol = conv.reshape(L, 4, 44, 128).transpose(3, 0, 2, 1).copy()
    wg = f(inputs["gla_w_gate"])
    bg = f(inputs["gla_b_gate"])
    wg_aug = np.zeros((L, 33, 256), np.float32)
    wg_aug[:, 0:16, 0:128] = wg[:, 0]
    wg_aug[:, 16:32, 128:256] = wg[:, 1]
    wg_aug[:, 32, 0:128] = bg[:, 0]
    wg_aug[:, 32, 128:256] = bg[:, 1]
    qn, kn = f(inputs["gqa_q_norm"]), f(inputs["gqa_k_norm"])
    gqa_qkn = np.concatenate([np.tile(qn, (1, 6)), np.tile(kn, (1, 2))], axis=1).copy()
    cos_t, sin_t = _rope_tables()
    s_idx = np.arange(128)[:, None]
    t_idx = np.arange(128)[None, :]
    NEG = np.float32(-30000.0)
    mb_prev = np.where(s_idx >= t_idx, np.float32(0), NEG).astype(np.float32)
    mb_next = np.where(s_idx <= t_idx, np.float32(0), NEG).astype(np.float32)
    shared = {
        "w_mod": f(inputs["w_mod"]), "b_mod": f(inputs["b_mod"]), "normcol": normcol,
        "attn_post_norm": f(inputs["attn_post_norm"]), "ffn_post_norm": f(inputs["ffn_post_norm"]),
        "w_in": f(inputs["w_in"]), "wg_aug": wg_aug, "gla_out_norm": f(inputs["gla_out_norm"]),
        "swa_sink": f(inputs["swa_sink"]), "gqa_qkn": gqa_qkn,
        "w_out": f(inputs["w_out"]), "ffn_w_up": f(inputs["ffn_w_up"]), "convcol": convcol,
        "ffn_w_down": f(inputs["ffn_w_down"]),
        "c_ident": np.eye(128, dtype=np.float32), "c_triu": np.triu(np.ones((128, 128), np.float32)),
        "c_tril": np.tril(np.ones((128, 128), np.float32)),
        "c_u16": (-np.triu(np.ones((128, 128), np.float32)) / 16.0).astype(np.float32),
        "c_l16": (-np.tril(np.ones((128, 128), np.float32)) / 16.0).astype(np.float32),
        "c_bm": (np.arange(128)[:, None] // 32 == np.arange(256)[None, :] // 64).astype(np.float32),
        "c_bmq": (np.arange(128)[:, None] // 32 == np.arange(512)[None, :] // 128).astype(np.float32),
        "c_mb_prev": np.tile(mb_prev, (1, 3)).copy(), "c_mb_next": np.tile(mb_next, (1, 3)).copy(),
        "c_cos": cos_t, "c_sin": sin_t,
    }
    maps = []
    for b in range(n_cores):
        ccol = np.stack([c[b].reshape(8, 128).T, c_ctx.reshape(8, 128).T], axis=-1).copy()
        m = dict(shared)
        m.update({"x": x[b], "ctx": ctx[b], "ccol": ccol})
        maps.append(m)
    return maps


def kernel(**inputs):
    b = Builder()
    nc = b.build()
    maps = make_in_maps(inputs)
    res = run_bass_kernel_spmd(nc, maps, core_ids=list(range(8)))
    return np.stack([np.asarray(r["y"], dtype=np.float32) for r in res.results], axis=0)
```

```python
import contextlib
import numpy as np
import concourse.bass as bass
import concourse.mybir as mybir
from concourse.bass_utils import run_bass_kernel_spmd

F32 = mybir.dt.float32
BF16 = mybir.dt.bfloat16
AF = mybir.ActivationFunctionType
ALU = mybir.AluOpType
AX = mybir.AxisListType

D = 1024
T_LAT = 2048
T_CTX = 256
NT = 18
NLT = 16
DEPTH = 2
FF = 2816
NFC = 22
IN_TOTAL = 2080
EPS = 1e-6


class Tok:
    __slots__ = ("name", "w", "wx", "r", "dsem", "dtotal", "excl")

    def __init__(self, name, excl=False):
        self.name = name
        self.excl = excl
        self.w = None
        self.wx = []
        self.r = {}
        self.dsem = None
        self.dtotal = 0


class Op:
    __slots__ = ("eng", "fn", "deps", "inc", "val", "is_dma", "dsem", "dval", "idx")

    def __init__(self, eng, fn, is_dma=False):
        self.eng = eng
        self.fn = fn
        self.deps = []
        self.inc = False
        self.val = 0
        self.is_dma = is_dma
        self.dsem = None
        self.dval = 0


class Sched:
    ENGS = ("pe", "act", "dve", "pool", "sp")

    def __init__(self, nc, es):
        self.nc = nc
        self.es = es
        self.ops = {e: [] for e in self.ENGS}
        self.sems = {e: es.enter_context(nc.semaphore("s_" + e)) for e in ("pe", "act", "dve", "pool")}
        self.n_dsem = 0
        self.last = {e: None for e in self.ENGS}
        self.dmas = []

    def full_barrier(self):
        lasts = [o for o in self.last.values() if o is not None]
        dm = list(self.dmas)
        for e in self.ENGS:
            o = Op(e, None)
            for ev in lasts:
                if not ev.is_dma:
                    ev.inc = True
                o.deps.append(ev)
            for ev in dm:
                o.deps.append(ev)
            self.ops[e].append(o)
        self.dmas = []

    def _dep(self, op, ev):
        if ev is None or ev is op:
            return
        if (not ev.is_dma) and ev.eng == "pe" and op.eng == "pe" and not op.is_dma:
            return
        if not ev.is_dma:
            ev.inc = True
        op.deps.append(ev)

    def _track(self, op, r, w, waw=True):
        key = ("d", id(op)) if op.is_dma else op.eng
        for t in r:
            self._dep(op, t.w)
            for ev in t.wx:
                self._dep(op, ev)
            if t.excl:
                for k2, ev in t.r.items():
                    if k2 != key:
                        self._dep(op, ev)
        for t in w:
            if waw:
                self._dep(op, t.w)
                for ev in t.wx:
                    self._dep(op, ev)
            for ev in t.r.values():
                self._dep(op, ev)
        for t in r:
            t.r[key] = op
        for t in w:
            if waw:
                t.wx = []
                t.r = {}
            elif t.w is not None:
                t.wx.append(t.w)
            t.w = op

    def op(self, eng, fn, r=(), w=(), waw=True):
        o = Op(eng, fn)
        self._track(o, r, w, waw=waw)
        self.ops[eng].append(o)
        self.last[eng] = o
        return o

    def pe(self, fn, r=(), w=()):
        return self.op("pe", fn, r, w)

    def act(self, fn, r=(), w=(), waw=True):
        return self.op("act", fn, r, w, waw)

    def dve(self, fn, r=(), w=(), waw=True):
        return self.op("dve", fn, r, w, waw)

    def pool(self, fn, r=(), w=()):
        return self.op("pool", fn, r, w)

    def dma(self, q, out, in_, r=(), w=(), waw=True, semtok=None):
        o = Op(q, lambda e: e.dma_start(out=out, in_=in_), is_dma=True)
        st = semtok if semtok is not None else w[0]
        if st.dsem is None:
            st.dsem = self.es.enter_context(self.nc.semaphore("d_%d" % self.n_dsem))
            self.n_dsem += 1
        st.dtotal += 16
        o.dsem = st.dsem
        o.dval = st.dtotal
        self._track(o, r, w, waw=waw)
        self.ops[q].append(o)
        self.dmas.append(o)
        return o

    def barrier_wait(self, eng, toks):
        o = Op(eng, None)
        for t in toks:
            self._dep(o, t.w)
        self.ops[eng].append(o)
        return o

    def emit(self, block):
        for e in ("pe", "act", "dve", "pool"):
            c = 0
            for o in self.ops[e]:
                if o.is_dma:
                    continue
                if o.inc:
                    c += 1
                    o.val = c
        handles = {"pe": block.tensor, "act": block.scalar, "dve": block.vector, "pool": block.gpsimd, "sp": block.sync}

        def mk(ename):
            ops = self.ops[ename]
            sems = self.sems

            def body(e):
                known = {}
                for o in ops:
                    for ev in o.deps:
                        if ev.is_dma:
                            sem, val = ev.dsem, ev.dval
                        else:
                            sem, val = sems[ev.eng], ev.val
                        k = id(sem)
                        if known.get(k, 0) >= val:
                            continue
                        e.wait_ge(sem, val)
                        known[k] = val
                    if o.fn is None:
                        continue
                    ins = o.fn(e)
                    if o.is_dma:
                        ins.then_inc(o.dsem, 16)
                    elif o.inc:
                        ins.then_inc(sems[ename], 1)
            return body

        for ename in self.ENGS:
            if self.ops[ename]:
                handles[ename](mk(ename))


def _rope_tables():
    t = np.arange(T_LAT)
    row = (t // 64).astype(np.float32)
    col = (t % 64).astype(np.float32)
    nf = 16
    inv = (np.float32(10000.0) ** (-np.arange(nf, dtype=np.float32) / np.float32(nf))).astype(np.float32)
    ar = row[:, None] * inv[None, :]
    ac = col[:, None] * inv[None, :]
    ang = np.concatenate([ar, ar, ac, ac], axis=-1).astype(np.float32)
    cos = np.cos(ang).astype(np.float32)
    sin = np.sin(ang).astype(np.float32)
    sgn = np.concatenate([-np.ones(16), np.ones(16), -np.ones(16), np.ones(16)]).astype(np.float32)
    sin2 = sin * sgn[None, :]
    cos_t = cos.reshape(NLT, 128, 64).transpose(1, 0, 2).copy()
    sin_t = sin2.reshape(NLT, 128, 64).transpose(1, 0, 2).copy()
    return cos_t, sin_t


class Region:
    def __init__(self, arena, base, size):
        self.arena, self.base, self.size, self.off = arena, base, size, 0

    def reset(self):
        self.off = 0

    def alloc(self, shape, dt):
        esz = 4 if dt == F32 else 2
        n = 1
        for d in shape[1:]:
            n *= d
        nbytes = n * esz
        start = self.base + self.off
        self.off += (nbytes + 63) // 64 * 64
        assert self.off <= self.size, "region overflow: need %d have %d" % (self.off, self.size)
        ap = self.arena[0:shape[0], start // 2:(start + nbytes) // 2]
        if dt == F32:
            ap = ap.bitcast(F32)
        if len(shape) > 2:
            names = ["d%d" % i for i in range(len(shape) - 1)]
            pat = "p (%s) -> p %s" % (" ".join(names), " ".join(names))
            ap = ap.rearrange(pat, **{nm: sz for nm, sz in zip(names[:-1], shape[1:-1])})
        return ap


ARENA_BYTES = 123 * 1024
R0_BYTES = 36992
R1_BYTES = 9216


class Builder:
    def __init__(self, n_layers=DEPTH, debug=(), stop_after=None):
        self.n_layers = n_layers
        self.debug = set(debug)
        self.stop_after = stop_after
        self.nc = bass.Bass("TRN2", target_bir_lowering=False)
        self.es = contextlib.ExitStack()
        self.dbg_outs = {}
        self.gla_stage = False

    def din(self, name, shape, dt=F32):
        return self.nc.dram_tensor(name, list(shape), dt, kind="ExternalInput").ap()

    def dout(self, name, shape, dt=F32):
        return self.nc.dram_tensor(name, list(shape), dt, kind="ExternalOutput").ap()

    def sb(self, name, shape, dt):
        return self.es.enter_context(self.nc.sbuf_tensor("sb_" + name, list(shape), dt))[:]

    def dump(self, name, src_ap, shape, r, dt=F32):
        if name not in self.debug:
            return
        o = self.dout("dbg_" + name, shape, dt)
        tok = Tok("dbg_" + name)
        self.S.dma("sp", o, src_ap, r=r, w=[tok])
        self.out_toks.append(tok)

    def nb(self, reg, shape, dt, n=2, name="t"):
        return [(reg.alloc(shape, dt), Tok(name + str(i))) for i in range(n)]

    def phase(self, *regs):
        self.S.full_barrier()
        for r in regs:
            r.reset()

    def build(self):
        nc, es = self.nc, self.es
        with es:
            self.S = S = Sched(nc, es)
            self.out_toks = []
            self.declare_io()
            self.alloc_persistent()
            self.load_consts()
            try:
                for l in range(self.n_layers):
                    self.layer(l)
                self.store_output()
            except StopIteration:
                pass
            S.barrier_wait("sp", self.out_toks)
            es.enter_context(nc.allow_low_precision("bf16 matmul operands / intermediates by design"))
            block = es.enter_context(nc.Block())
            S.emit(block)
        return nc

    def stop(self, tag):
        if self.stop_after == tag:
            raise StopIteration

    def declare_io(self):
        L = DEPTH
        self.x_in = self.din("x", [T_LAT, D])
        self.ctx_in = self.din("ctx", [T_CTX, D])
        self.ccol_in = self.din("ccol", [128, 8, 2])
        self.w_mod = self.din("w_mod", [L, D, 6 * D])
        self.b_mod = self.din("b_mod", [L, 6 * D])
        self.normcol_in = self.din("normcol", [128, L, 2, 8])
        self.attn_post = self.din("attn_post_norm", [L, D])
        self.ffn_post = self.din("ffn_post_norm", [L, D])
        self.w_in = self.din("w_in", [L, D, IN_TOTAL])
        self.wg_aug = self.din("wg_aug", [L, 33, 256])
        self.gla_onorm = self.din("gla_out_norm", [L, 64])
        self.swa_sink = self.din("swa_sink", [L, 6])
        self.gqa_qkn = self.din("gqa_qkn", [L, 512])
        self.w_out = self.din("w_out", [L, D, D])
        self.w_up = self.din("ffn_w_up", [L, D, 2 * FF])
        self.convcol_in = self.din("convcol", [128, L, 44, 4])
        self.w_down = self.din("ffn_w_down", [L, FF, D])
        self.ident_in = self.din("c_ident", [128, 128])
        self.triu_in = self.din("c_triu", [128, 128])
        self.tril_in = self.din("c_tril", [128, 128])
        self.u16_in = self.din("c_u16", [128, 128])
        self.l16_in = self.din("c_l16", [128, 128])
        self.bm_in = self.din("c_bm", [128, 256])
        self.bmq_in = self.din("c_bmq", [128, 512])
        self.mbp_in = self.din("c_mb_prev", [128, 384])
        self.mbn_in = self.din("c_mb_next", [128, 384])
        self.cos_in = self.din("c_cos", [128, NLT, 64])
        self.sin_in = self.din("c_sin", [128, NLT, 64])
        self.y_out = self.dout("y", [T_LAT, D])
        self.wup_bf = self.nc.dram_tensor("wup_bf", [NFC // 2, 128, 8, 2, 256], BF16, kind="Internal").ap()
        self.wd_bf = self.nc.dram_tensor("wd_bf", [128, NFC, D], BF16, kind="Internal").ap()
        self.wupbf_t = [Tok("wupbf%d" % j) for j in range(NFC // 2)]
        self.wdbf_t = Tok("wdbf")

    def alloc_persistent(self):
        self.X = self.sb("X", [128, NT, D], F32)
        self.xt = [Tok("x%d" % j) for j in range(NT)]
        self.psall = self.es.enter_context(self.nc.psum_tensor("psall", [128, 8, 512], F32))[:]
        self.ps = [self.psall[:, i, :] for i in range(8)]
        self.pst = [Tok("ps%d" % i, excl=True) for i in range(8)]
        self.ident_f = self.sb("ident_f", [128, 128], F32)
        self.ident_b = self.sb("ident_b", [128, 128], BF16)
        self.cosb = self.sb("cosb", [128, NLT, 64], BF16)
        self.sinb = self.sb("sinb", [128, NLT, 64], BF16)
        self.cst = Tok("consts")
        self.cst2 = Tok("consts2")
        self.epsc = self.sb("epsc", [128, 1], F32)
        self.onec = self.sb("onec", [128, 1], F32)
        self.ccol = self.sb("ccol", [128, 8, 2], F32)
        self.normcol = self.sb("normcol", [128, DEPTH, 2, 8], F32)
        self.convcol = self.sb("convcol", [128, DEPTH, 44, 4], F32)
        self.sel = self.sb("sel", [2, 2, 128], BF16)
        self.grow = self.sb("grow", [2, 2, D], BF16)
        self.grow_t = Tok("grow")
        self.modcol = self.sb("modcol", [128, 4, 8, 2], F32)
        self.modcol_t = Tok("modcol")
        self.ABcol = self.sb("ABcol", [128, 2, 2, 8, 2], F32)
        self.ABcol_t = Tok("ABcol")
        self.ms = self.sb("ms", [128, NT], F32)
        self.rstd = self.sb("rstd", [128, NT], F32)
        self.ms_t = Tok("ms")
        self.arena = self.sb("arena", [128, ARENA_BYTES // 2], BF16)
        self.R0 = Region(self.arena, 0, R0_BYTES)
        self.R1 = Region(self.arena, R0_BYTES, R1_BYTES)
        self.R2 = Region(self.arena, R0_BYTES + R1_BYTES, ARENA_BYTES - R0_BYTES - R1_BYTES)
        self.RALL = Region(self.arena, 0, ARENA_BYTES)

    def load_consts(self):
        S = self.S
        for j in range(NLT):
            S.dma("sp", self.X[:, j, :], self.x_in[j * 128:(j + 1) * 128, :], w=[self.xt[j]])
        for j in range(2):
            S.dma("sp", self.X[:, NLT + j, :], self.ctx_in[j * 128:(j + 1) * 128, :], w=[self.xt[NLT + j]])
        c = self.cst
        S.dma("sp", self.ident_f, self.ident_in[:, :], w=[c], waw=False)
        S.dma("pool", self.ident_b, self.ident_in[:, :], w=[c], waw=False)
        S.dma("pool", self.cosb, self.cos_in[:, :, :], w=[c], waw=False)
        S.dma("pool", self.sinb, self.sin_in[:, :, :], w=[c], waw=False)
        S.dma("sp", self.ccol, self.ccol_in[:, :, :], w=[c], waw=False)
        S.dma("sp", self.normcol, self.normcol_in[:, :, :, :], w=[c], waw=False)
        S.dma("sp", self.convcol, self.convcol_in[:, :, :, :], w=[c], waw=False)
        S.dve(lambda e: e.memset(self.epsc, EPS), w=[self.cst2])
        S.dve(lambda e: e.memset(self.onec, 1.0), w=[self.cst2])
        S.dve(lambda e: e.memset(self.sel, 0.0), w=[self.cst2])
        S.dve(lambda e: e.memset(self.sel[0:1, 0, :], 1.0), w=[self.cst2])
        S.dma("pool", self.sel[1:2, 1, :], self.triu_in[0:1, :], w=[self.cst2])

    def store_output(self):
        S = self.S
        for j in range(NLT):
            t = Tok("y%d" % j)
            S.dma("sp", self.y_out[j * 128:(j + 1) * 128, :], self.X[:, j, :], r=[self.xt[j]], w=[t])
            self.out_toks.append(t)

    def layer(self, l):
        need_ctx = l < DEPTH - 1
        self.phase(self.RALL, self.R0, self.R1, self.R2)
        self.mod_vectors(l)
        self.stop("mod")
        self.phase(self.R2)
        self.HTW = 1 + T_LAT + 2 + T_CTX + 1
        self.hT = self.R0.alloc([128, 8, self.HTW], BF16)
        self.hT_t = [Tok("hT%d" % j) for j in range(NT)]
        self.hT_pad = Tok("hTpad")
        self.wing = self.R2.alloc([128, 8, 800], BF16)
        self.wing_t = Tok("wing")
        for k in range(8):
            self.S.dma("pool", self.wing[:, k, :], self.w_in[l, k * 128:(k + 1) * 128, 0:800], w=[self.wing_t], waw=False)
        self.norm_stats(list(range(NT)))
        items = [(n, 0, 128, self.hcol(n), self.hT_t[n]) for n in range(NT)]
        self.norm_tiles(l, 0, items, self.hT, self.R2)
        self.dump("hT_l%d" % l, self.hT.rearrange("p c t -> p (c t)"), [128, 8 * self.HTW], r=self.hT_t, dt=BF16)
        self.stop("norm1")
        self.gla_phase(l, need_ctx)
        self.stop("gla")
        self.phase(self.R2)
        self.kvq_phase(l, need_ctx)
        self.stop("kvq")
        self.phase(self.R0)
        self.attn_phase(l, need_ctx)
        self.stop("attn")
        self.phase(self.RALL, self.R0, self.R1, self.R2)
        self.ffn_phase(l, need_ctx)
        self.stop("ffn")

    def run_interleaved(self, gens, width):
        gens = list(gens)
        active = []
        while gens or active:
            while gens and len(active) < width:
                active.append(gens.pop(0))
            for g in list(active):
                try:
                    next(g)
                except StopIteration:
                    active.remove(g)

    def run_staged(self, gens):
        gens = list(gens)

        def run_A(g):
            for m in g:
                if m == "A_done":
                    return

        cur = gens.pop(0)
        run_A(cur)
        while cur is not None:
            nxt = gens.pop(0) if gens else None
            cur_done, nxt_done = False, nxt is None
            while not (cur_done and nxt_done):
                if not cur_done:
                    try:
                        next(cur)
                    except StopIteration:
                        cur_done = True
                if not nxt_done:
                    try:
                        if next(nxt) == "A_done":
                            nxt_done = True
                    except StopIteration:
                        nxt_done = True
            cur = nxt

    def hcol(self, n):
        return (1 + n * 128) if n < NLT else (1 + T_LAT + 2 + (n - NLT) * 128)

    def mod_vectors(self, l):
        S = self.S
        R = self.R2
        scT = R.alloc([128, 8, 2], BF16)
        sc_tmp = R.alloc([128, 16], F32)
        sc_t = Tok("scT")
        wm = self.nb(R, [128, 8, 512], F32, 3, "wm")
        wmb16 = self.nb(R, [128, 8, 512], BF16, 2, "wmb")
        bm = self.nb(R, [2, 512], F32, 2, "bm")
        modp = self.nb(R, [2, 512], F32, 2, "modp")
        ccf = self.ccol.rearrange("p c t -> p (c t)")
        S.act(lambda e: e.activation(out=sc_tmp, in_=ccf, func=AF.Exp, scale=-1.0), r=[self.cst], w=[sc_t])
        S.dve(lambda e: e.tensor_scalar_add(out=sc_tmp, in0=sc_tmp, scalar1=1.0), r=[sc_t], w=[sc_t])
        S.dve(lambda e: e.reciprocal(out=sc_tmp, in_=sc_tmp), r=[sc_t], w=[sc_t])
        S.dve(lambda e: e.tensor_tensor(out=scT.rearrange("p c t -> p (c t)"), in0=sc_tmp, in1=ccf, op=ALU.mult),
              r=[sc_t, self.cst], w=[sc_t])
        ps, pst = self.ps, self.pst
        for pc in range(12):
            sl = pc % 2
            ch, half = pc // 2, pc % 2
            wmb, wt = wm[pc % 3]
            bmb, bmt = bm[sl]
            mp, mt = modp[sl]
            S.dma("sp", wmb, self.w_mod[l, :, pc * 512:(pc + 1) * 512].rearrange("(k p) n -> p k n", p=128), w=[wt])
            S.dma("sp", bmb, self.b_mod[l, pc * 512:(pc + 1) * 512].partition_broadcast(2), w=[bmt])
            w16, w16t = wmb16[sl]
            S.act(lambda e, wmb=wmb, w16=w16: e.activation(out=w16[:, 0:4, :], in_=wmb[:, 0:4, :], func=AF.Copy), r=[wt], w=[w16t])
            S.dve(lambda e, wmb=wmb, w16=w16: e.tensor_copy(out=w16[:, 4:7, :], in_=wmb[:, 4:7, :]), r=[wt], w=[w16t], waw=False)
            S.op("pool", lambda e, wmb=wmb, w16=w16: e.tensor_copy(out=w16[:, 7:8, :], in_=wmb[:, 7:8, :]), r=[wt], w=[w16t], waw=False)
            for k in range(8):
                S.pe(lambda e, k=k, w16=w16: e.matmul(ps[0][0:2, :], scT[:, k, :], w16[:, k, :], start=(k == 0), stop=(k == 7)),
                     r=[sc_t, w16t], w=[pst[0]])
            S.dve(lambda e, mp=mp, bmb=bmb: e.tensor_tensor(out=mp, in0=ps[0][0:2, :], in1=bmb, op=ALU.add), r=[pst[0], bmt], w=[mt])
            if ch in (2, 5):
                wh = 0 if ch == 2 else 1
                S.act(lambda e, mp=mp, wh=wh, half=half: e.activation(out=self.grow[:, wh, half * 512:(half + 1) * 512], in_=mp, func=AF.Copy),
                      r=[mt], w=[self.grow_t])
            else:
                slot = {0: 0, 1: 1, 3: 2, 4: 3}[ch]
                for i in range(4):
                    S.pe(lambda e, mp=mp, i=i: e.transpose(ps[1][:, i * 2:(i + 1) * 2], mp[:, i * 128:(i + 1) * 128], self.ident_f[0:2, 0:2]),
                         r=[mt, self.cst], w=[pst[1]])
                S.dve(lambda e, slot=slot, half=half: e.tensor_copy(
                    out=self.modcol[:, slot, half * 4:(half + 1) * 4, :], in_=ps[1][:, 0:8].rearrange("p (c t) -> p c t", t=2)),
                    r=[pst[1]], w=[self.modcol_t])
        for wn in range(2):
            nw = self.normcol[:, l, wn, :].unsqueeze(2).broadcast_to([128, 8, 2])
            S.dve(lambda e, wn=wn, nw=nw: e.scalar_tensor_tensor(
                out=self.ABcol[:, wn, 0, :, :], in0=self.modcol[:, 2 * wn + 1, :, :], scalar=1.0, in1=nw, op0=ALU.add, op1=ALU.mult),
                r=[self.modcol_t, self.cst], w=[self.ABcol_t])
            S.dve(lambda e, wn=wn: e.tensor_copy(out=self.ABcol[:, wn, 1, :, :], in_=self.modcol[:, 2 * wn, :, :]),
                  r=[self.modcol_t], w=[self.ABcol_t])

    def norm_stats(self, tiles):
        S = self.S
        junk = self.R2.alloc([128, D], BF16)
        jt = Tok("sqjunk")
        for j in tiles:
            S.act(lambda e, j=j: e.activation(out=junk, in_=self.X[:, j, :], func=AF.Square, scale=1.0 / 32.0,
                                              accum_out=self.ms[:, j:j + 1]), r=[self.xt[j]], w=[jt, self.ms_t])
        S.act(lambda e: e.activation(out=self.rstd, in_=self.ms, func=AF.Ln, bias=self.epsc[:, 0:1]), r=[self.ms_t, self.cst2], w=[self.ms_t])
        S.act(lambda e: e.activation(out=self.rstd, in_=self.rstd, func=AF.Exp, scale=-0.5), r=[self.ms_t], w=[self.ms_t])

    def norm_tiles(self, l, which, items, dest, reg):
        S = self.S
        xn = self.nb(reg, [128, D], BF16, 2, "xn")
        for i, (n, lo, hi, dc, dtok) in enumerate(items):
            sl = i % 2
            lc = 0 if n < NLT else 1
            xnb, xnt = xn[sl]
            bank, bt = self.ps[2 + sl], self.pst[2 + sl]
            S.dve(lambda e, n=n, xnb=xnb: e.tensor_scalar_mul(out=xnb, in0=self.X[:, n, :], scalar1=self.rstd[:, n:n + 1]),
                  r=[self.xt[n], self.ms_t], w=[xnt])
            pb = bank.bitcast(BF16)
            for c in range(8):
                S.pe(lambda e, c=c, xnb=xnb, pb=pb: e.transpose(pb[:, c * 128:(c + 1) * 128], xnb[:, c * 128:(c + 1) * 128], self.ident_b),
                     r=[xnt, self.cst], w=[bt])
            w = hi - lo
            for c in range(8):
                if c < 4:
                    S.act(lambda e, c=c, pb=pb, lo=lo, hi=hi, dc=dc, w=w, lc=lc: e.activation(
                        out=dest[:, c, dc:dc + w], in_=pb[:, c * 128 + lo:c * 128 + hi], func=AF.Identity,
                        scale=self.ABcol[:, which, 0, c, lc:lc + 1], bias=self.ABcol[:, which, 1, c, lc:lc + 1]), r=[bt, self.ABcol_t], w=[dtok], waw=False)
                else:
                    S.dve(lambda e, c=c, pb=pb, lo=lo, hi=hi, dc=dc, w=w, lc=lc: e.tensor_scalar(
                        out=dest[:, c, dc:dc + w], in0=pb[:, c * 128 + lo:c * 128 + hi],
                        scalar1=self.ABcol[:, which, 0, c, lc:lc + 1], scalar2=self.ABcol[:, which, 1, c, lc:lc + 1],
                        op0=ALU.mult, op1=ALU.add), r=[bt, self.ABcol_t], w=[dtok], waw=False)

    def gla_phase(self, l, need_ctx):
        S = self.S
        R = self.R2
        ps, pst = self.ps, self.pst
        gc_t = Tok("gla_consts")
        triu_f = R.alloc([128, 128], F32)
        tril_f = R.alloc([128, 128], F32)
        u16 = R.alloc([128, 128], BF16)
        l16 = R.alloc([128, 128], BF16)
        bm_f = R.alloc([128, 256], F32)
        wg_sb = R.alloc([33, 256], BF16)
        onorm = R.alloc([128, 64], F32)
        for dst, src in ((triu_f, self.triu_in), (tril_f, self.tril_in), (bm_f, self.bm_in)):
            S.dma("sp", dst, src[:, :], w=[gc_t], waw=False)
        for dst, src in ((u16, self.u16_in), (l16, self.l16_in)):
            S.dma("pool", dst, src[:, :], w=[gc_t], waw=False)
        S.dma("pool", wg_sb, self.wg_aug[l, :, :], w=[gc_t], waw=False)
        S.dma("sp", onorm, self.gla_onorm[l, :].partition_broadcast(128), w=[gc_t], waw=False)
        wing, wing_t = self.wing, self.wing_t
        zsb = self.nb(R, [128, 32], BF16, 2, "zsb")
        zT = self.nb(R, [33, 128], BF16, 2, "zT")
        sp1 = self.nb(R, [128, 256], F32, 2, "sp1")
        sp1b = self.nb(R, [128, 256], BF16, 2, "sp1b")
        expB = self.nb(R, [128, 256], F32, 2, "expB")
        expNB = self.nb(R, [128, 256], F32, 2, "expNB")
        qtb = self.nb(R, [128, 2, 128], BF16, 2, "qtb")
        ktb = self.nb(R, [128, 2, 128], BF16, 2, "ktb")
        vbf = self.nb(R, [128, 256], BF16, 2, "vbf")
        kTt = self.nb(R, [128, 2, 128], BF16, 2, "kTt")
        bdq = self.nb(R, [128, 2, 512], BF16, 1, "bdq")
        attm = self.nb(R, [128, 2, 512], BF16, 1, "attm")
        ger = self.nb(R, [128, 256], F32, 2, "ger")
        qTs = R.alloc([128, NT, 2, 128], BF16)
        qTs_t = [Tok("qTs%d" % n) for n in range(NT)]
        KVS = R.alloc([128, NT, 2, 64], BF16)
        tkv = self.nb(R, [128, 256], F32, 2, "tkv")
        bdS = self.nb(R, [128, 256], BF16, 2, "bdS")
        KVS_t = [[Tok("KVS%d_%d" % (n, d)) for d in range(2)] for n in range(NT)]
        Eall = R.alloc([128, NT, 2], F32)
        E_t = [Tok("E%d" % n) for n in range(NT)]
        gate = R.alloc([128, NT, 256], BF16)
        gate_t = [Tok("gate%d" % n) for n in range(NT)]
        Rr = [self.nb(R, [128, 64], F32, 2, "R%d_" % d) for d in range(2)]
        go = self.nb(R, [128, 256], F32, 2, "go")
        gsq = self.nb(R, [128, 256], F32, 2, "gsq")
        gss = self.nb(R, [128, 4], F32, 2, "gss")
        self.oloc = self.R1.alloc([128, NT, 256], BF16)
        self.oloc_t = [Tok("oloc%d" % n) for n in range(NT)]
        oloc, oloc_t = self.oloc, self.oloc_t
        for (t, k) in zT:
            S.dve(lambda e, t=t: e.memset(t[32:33, :], 1.0), w=[k])
        bmq_f = R.alloc([128, 512], BF16)
        S.dma("pool", bmq_f, self.bmq_in[:, :], w=[gc_t], waw=False)
        qk_sb = self.nb(R, [128, 256], F32, 2, "qk_sb")

        def gla_proj(n):
            c0 = self.hcol(n)
            for (bank, lo, hi) in ((0, 0, 512), (1, 512, 800)):
                for k in range(8):
                    S.pe(lambda e, k=k, bank=bank, lo=lo, hi=hi: e.matmul(
                        ps[bank][:, 0:hi - lo], self.hT[:, k, c0:c0 + 128], wing[:, k, lo:hi], start=(k == 0), stop=(k == 7)),
                        r=[self.hT_t[n], wing_t], w=[pst[bank]])

        def tile_gen(n):
            sl = n % 2
            zs, zs_t = zsb[sl]
            zt, zt_t = zT[sl]
            S.act(lambda e: e.activation(out=zs, in_=ps[1][:, 256:288], func=AF.Copy), r=[pst[1]], w=[zs_t])
            yield
            pb2 = ps[2].bitcast(BF16)
            S.pe(lambda e: e.transpose(pb2[0:32, 0:128], zs, self.ident_b), r=[zs_t, self.cst], w=[pst[2]])
            qk, qk_t = qk_sb[sl]
            vb, vb_t = vbf[sl]
            S.act(lambda e: e.activation(out=qk, in_=ps[0][:, 0:256], func=AF.Copy), r=[pst[0]], w=[qk_t])
            S.act(lambda e: e.activation(out=vb, in_=ps[0][:, 256:512], func=AF.Copy), r=[pst[0]], w=[vb_t])
            yield
            S.dve(lambda e: e.tensor_copy(out=zt[0:32, :], in_=pb2[0:32, 0:128]), r=[pst[2]], w=[zt_t])
            yield
            S.pe(lambda e: e.matmul(ps[2][:, 128:384], zt, wg_sb, start=True, stop=True), r=[zt_t, gc_t], w=[pst[2]])
            gr, gr_t = ger[sl]
            S.act(lambda e: e.activation(out=gr, in_=ps[1][:, 0:256], func=AF.Exp, scale=-1.0), r=[pst[1]], w=[gr_t])
            yield
            s1, s1_t = sp1[sl]
            S.act(lambda e: e.activation(out=s1, in_=ps[2][:, 128:384], func=AF.Exp, scale=-1.0), r=[pst[2]], w=[s1_t])
            S.dve(lambda e: e.tensor_scalar_add(out=gr, in0=gr, scalar1=1.0), r=[gr_t], w=[gr_t])
            yield
            s1b, s1b_t = sp1b[sl]
            S.act(lambda e: e.activation(out=s1b, in_=s1, func=AF.Ln, bias=self.onec[:, 0:1]), r=[s1_t, self.cst2], w=[s1b_t])
            S.dve(lambda e: e.reciprocal(out=gr, in_=gr), r=[gr_t], w=[gr_t])
            yield
            S.pe(lambda e: e.matmul(ps[3][:, 0:128], u16, s1b[:, 0:128], start=True, stop=True), r=[s1b_t, gc_t], w=[pst[3]])
            S.pe(lambda e: e.matmul(ps[3][:, 128:256], l16, s1b[:, 128:256], start=True, stop=True), r=[s1b_t, gc_t], w=[pst[3]])
            S.pe(lambda e: e.matmul(ps[2][:, 384:386], s1b[:, 0:128], u16[:, 126:128], start=True, stop=True), r=[s1b_t, gc_t], w=[pst[2]])
            S.pe(lambda e: e.matmul(ps[2][:, 386:388], s1b[:, 128:256], u16[:, 126:128], start=True, stop=True), r=[s1b_t, gc_t], w=[pst[2]])
            S.dve(lambda e: e.tensor_tensor(out=gate[:, n, :], in0=ps[1][:, 0:256], in1=gr, op=ALU.mult), r=[pst[1], gr_t], w=[gate_t[n]])
            yield
            eB, eB_t = expB[sl]
            eN, eN_t = expNB[sl]
            S.act(lambda e: e.activation(out=eB, in_=ps[3][:, 0:256], func=AF.Exp), r=[pst[3]], w=[eB_t])
            yield
            S.act(lambda e: e.activation(out=eN, in_=ps[3][:, 0:256], func=AF.Exp, scale=-1.0), r=[pst[3]], w=[eN_t])
            S.act(lambda e: e.activation(out=Eall[:, n, :], in_=ps[2][:, 385:388:2], func=AF.Exp), r=[pst[2]], w=[E_t[n]])
            yield "A_done"
            qb, qb_t = qtb[sl]
            kb, kb_t = ktb[sl]
            S.dve(lambda e: e.scalar_tensor_tensor(
                out=qb, in0=eB.rearrange("p (d f) -> p d f", d=2), scalar=32.0 ** -0.5,
                in1=qk[:, 0:128].unsqueeze(1).broadcast_to([128, 2, 128]), op0=ALU.mult, op1=ALU.mult), r=[qk_t, eB_t], w=[qb_t])
            S.dve(lambda e: e.tensor_tensor(
                out=kb, in0=eN.rearrange("p (d f) -> p d f", d=2),
                in1=qk[:, 128:256].unsqueeze(1).broadcast_to([128, 2, 128]), op=ALU.mult), r=[qk_t, eN_t], w=[kb_t])
            yield
            pb4 = ps[4].bitcast(BF16)
            for i, (src, d) in enumerate(((qb, 0), (qb, 1), (kb, 0), (kb, 1))):
                S.pe(lambda e, i=i, src=src, d=d: e.transpose(pb4[:, i * 128:(i + 1) * 128], src[:, d, :], self.ident_b),
                     r=[qb_t if i < 2 else kb_t, self.cst], w=[pst[4]])
            yield
            kt, kt_t = kTt[sl]
            S.dve(lambda e: e.tensor_copy(out=qTs[:, n, :, :], in_=pb4[:, 0:256].rearrange("p (d t) -> p d t", d=2)), r=[pst[4]], w=[qTs_t[n]])
            S.act(lambda e: e.activation(out=kt, in_=pb4[:, 256:512].rearrange("p (d t) -> p d t", d=2), func=AF.Copy), r=[pst[4]], w=[kt_t])
            yield
            bq, bq_t = bdq[0]
            am, am_t = attm[0]
            for d in range(2):
                S.dve(lambda e, d=d: e.tensor_tensor(
                    out=bq[:, d, :].rearrange("p (h t) -> p h t", h=4),
                    in0=bmq_f.rearrange("p (h t) -> p h t", h=4),
                    in1=qTs[:, n, d, :].unsqueeze(1).broadcast_to([128, 4, 128]), op=ALU.mult), r=[qTs_t[n], gc_t], w=[bq_t])
            yield
            for d in range(2):
                S.pe(lambda e, d=d: e.matmul(ps[5 + d], kt[:, d, :], bq[:, d, :], start=True, stop=True),
                     r=[kt_t, bq_t], w=[pst[5 + d]])
            yield
            for d in range(2):
                msk = triu_f if d == 0 else tril_f
                S.dve(lambda e, d=d, msk=msk: e.tensor_tensor(
                    out=am[:, d, :].rearrange("p (h t) -> p h t", h=4), in0=ps[5 + d].rearrange("p (h t) -> p h t", h=4),
                    in1=msk.unsqueeze(1).broadcast_to([128, 4, 128]), op=ALU.mult), r=[pst[5 + d], gc_t], w=[am_t])
                yield
            for h in range(4):
                for d in range(2):
                    S.pe(lambda e, h=h, d=d: e.matmul(
                        ps[4][:, 256 + h * 64:256 + (h + 1) * 64], am[:, d, h * 128:(h + 1) * 128], vb[:, h * 64:(h + 1) * 64],
                        start=(d == 0), stop=(d == 1)), r=[am_t, vb_t], w=[pst[4]])
            for d in range(2):
                S.pe(lambda e, d=d: e.matmul(ps[7][:, d * 256:(d + 1) * 256], kb[:, d, :], vb, start=True, stop=True),
                     r=[kb_t, vb_t], w=[pst[7]])
            yield
            S.act(lambda e: e.activation(out=oloc[:, n, :], in_=ps[4][:, 256:512], func=AF.Copy), r=[pst[4]], w=[oloc_t[n]])
            for d in range(2):
                tk, tk_t = tkv[d]
                S.dve(lambda e, d=d, tk=tk: e.scalar_tensor_tensor(
                    out=tk, in0=ps[7][:, d * 256:(d + 1) * 256], scalar=Eall[:, n, d:d + 1], in1=bm_f,
                    op0=ALU.mult, op1=ALU.mult), r=[pst[7], E_t[n], gc_t], w=[tk_t])
                yield
                S.dve(lambda e, d=d, tk=tk: e.tensor_reduce(out=KVS[:, n, d, :], in_=tk.rearrange("p (h d) -> p d h", h=4), axis=AX.X, op=ALU.add),
                      r=[tk_t], w=[KVS_t[n][d]])
            yield

        for n in range(NT):
            gla_proj(n)
            for m in tile_gen(n):
                pass
        self.stop("g10")
        orders = ([16, 17] + list(range(16)), [17, 16] + list(range(15, -1, -1)))
        for d in range(2):
            S.dve(lambda e, d=d: e.memset(Rr[d][0][0], 0.0), w=[Rr[d][0][1]])
        for i in range(NT):
            for d in range(2):
                n = orders[d][i]
                Rc, Rc_t = Rr[d][i % 2]
                Rn, Rn_t = Rr[d][(i + 1) % 2]
                S.dve(lambda e, d=d, n=n, Rc=Rc, Rn=Rn: e.scalar_tensor_tensor(
                    out=Rn, in0=Rc, scalar=Eall[:, n, d:d + 1], in1=KVS[:, n, d, :], op0=ALU.mult, op1=ALU.add),
                    r=[Rc_t, E_t[n], KVS_t[n][d]], w=[Rn_t])
                S.dve(lambda e, d=d, n=n, Rc=Rc: e.tensor_copy(out=KVS[:, n, d, :], in_=Rc), r=[Rc_t], w=[KVS_t[n][d]])
        self.stop("g11")
        tiles = list(range(NT)) if need_ctx else list(range(NLT))
        bdS2 = [bdS, self.nb(R, [128, 256], BF16, 2, "bdSb")]

        def pass2_gen(n):
            sl = n % 2
            bank = sl
            for d in range(2):
                bd, bd_t = bdS2[sl][d]
                S.dve(lambda e, d=d, bd=bd: e.tensor_tensor(out=bd.rearrange("p (h d) -> p h d", h=4),
                                                            in0=bm_f.rearrange("p (h d) -> p h d", h=4),
                                                            in1=KVS[:, n, d, :].unsqueeze(1).broadcast_to([128, 4, 64]), op=ALU.mult),
                      r=[KVS_t[n][d], gc_t], w=[bd_t])
                S.pe(lambda e, d=d, bd=bd: e.matmul(ps[bank][:, 0:256], qTs[:, n, d, :], bd, start=(d == 0), stop=(d == 1)),
                     r=[qTs_t[n], bd_t], w=[pst[bank]])
                yield
            g, g_t = go[sl]
            q2, q2_t = gsq[sl]
            ss, ss_t = gss[sl]
            S.dve(lambda e: e.tensor_tensor(out=g, in0=ps[bank][:, 0:256], in1=oloc[:, n, :], op=ALU.add),
                  r=[pst[bank], oloc_t[n]], w=[g_t])
            yield
            S.act(lambda e: e.activation(out=q2, in_=g, func=AF.Square), r=[g_t], w=[q2_t])
            yield
            S.dve(lambda e: e.tensor_reduce(out=ss, in_=q2.rearrange("p (h d) -> p h d", h=4), axis=AX.X, op=ALU.add),
                  r=[q2_t], w=[ss_t])
            yield
            S.act(lambda e: e.activation(out=ss, in_=ss, func=AF.Ln, scale=1.0 / 64.0, bias=self.epsc[:, 0:1]), r=[ss_t, self.cst2], w=[ss_t])
            yield
            S.act(lambda e: e.activation(out=ss, in_=ss, func=AF.Exp, scale=-0.5), r=[ss_t], w=[ss_t])
            yield
            g3 = g.rearrange("p (h d) -> p h d", h=4)
            S.pool(lambda e: e.tensor_tensor(out=g3, in0=g3, in1=ss.unsqueeze(2).broadcast_to([128, 4, 64]), op=ALU.mult),
                   r=[g_t, ss_t], w=[g_t])
            yield
            S.pool(lambda e: e.tensor_tensor(out=g3, in0=g3, in1=onorm.unsqueeze(1).broadcast_to([128, 4, 64]), op=ALU.mult),
                   r=[g_t, gc_t], w=[g_t])
            yield
            S.pool(lambda e: e.tensor_tensor(out=oloc[:, n, :], in0=g, in1=gate[:, n, :], op=ALU.mult),
                   r=[g_t, gate_t[n]], w=[oloc_t[n]])
            yield

        self.run_interleaved([pass2_gen(n) for n in tiles], 2)
        self.dump("gla_l%d" % l, oloc.rearrange("p n f -> p (n f)"), [128, NT * 256], r=oloc_t, dt=BF16)

    def kvq_phase(self, l, need_ctx):
        S = self.S
        R = self.R2
        ps, pst = self.ps, self.pst
        wq = R.alloc([128, 8, 1280], BF16)
        wq_t = Tok("wq")
        for k in range(8):
            S.dma("pool", wq[:, k, :], self.w_in[l, k * 128:(k + 1) * 128, 800:2080], w=[wq_t], waw=False)
        self.q_tok = R.alloc([128, NT, 768], BF16)
        self.q_t = [Tok("q%d" % n) for n in range(NT)]
        self.kT = R.alloc([128, 2, NT * 128], BF16)
        self.kT_t = [Tok("kT%d" % n) for n in range(NT)]
        self.V = R.alloc([128, NT, 4, 65], BF16)
        self.V_t = [Tok("V%d" % n) for n in range(NT)]
        vones = Tok("vones")
        wqk = R.alloc([128, 512], BF16)
        wqk_t = Tok("wqk")
        S.dma("pool", wqk, self.gqa_qkn[l, :].partition_broadcast(128), w=[wqk_t])
        S.dve(lambda e: e.memset(self.V[:, :, :, 64:65], 1.0), w=[vones])
        sq = self.nb(R, [128, 512], BF16, 1, "sq") * 2
        ss8 = self.nb(R, [128, 8], F32, 2, "ss8")
        nq = self.nb(R, [128, 512], F32, 2, "nq")
        t1s = self.nb(R, [128, 512], BF16, 1, "t1s")[0]
        t2s = self.nb(R, [128, 512], BF16, 1, "t2s")[0]
        t1g = self.nb(R, [128, 512], BF16, 1, "t1g") * 2
        t2g = self.nb(R, [128, 512], BF16, 1, "t2g") * 2
        kr = self.nb(R, [128, 256], BF16, 2, "kr")

        def rope(eng, n, src, src_t, a, a_t, b, b_t, qoff, koff, krb, kr_t):
            s5 = src.rearrange("p (h r f s) -> p h r f s", h=8, r=2, f=2)
            b5 = b.rearrange("p (h r f s) -> p h r f s", h=8, r=2, f=2)
            sn5 = self.sinb[:, n, :].rearrange("p (r f s) -> p r f s", r=2, f=2)
            cosv = self.cosb[:, n, :].unsqueeze(1).broadcast_to([128, 8, 64])
            eng(lambda e: e.tensor_tensor(out=a.rearrange("p (h d) -> p h d", h=8), in0=src.rearrange("p (h d) -> p h d", h=8), in1=cosv, op=ALU.mult),
                r=[src_t, self.cst], w=[a_t])
            for f in range(2):
                eng(lambda e, f=f: e.tensor_tensor(out=b5[:, :, :, f, :], in0=s5[:, :, :, 1 - f, :],
                                                   in1=sn5[:, :, f, :].unsqueeze(1).broadcast_to([128, 8, 2, 16]), op=ALU.mult),
                    r=[src_t, self.cst], w=[b_t])
            eng(lambda e: e.tensor_tensor(out=self.q_tok[:, n, qoff:qoff + 384], in0=a[:, 0:384], in1=b[:, 0:384], op=ALU.add),
                r=[a_t, b_t], w=[self.q_t[n]])
            eng(lambda e: e.tensor_tensor(out=krb[:, koff:koff + 128], in0=a[:, 384:512], in1=b[:, 384:512], op=ALU.add),
                r=[a_t, b_t], w=[kr_t])

        def proj(n):
            sl = n % 2
            c0 = self.hcol(n)
            bA, bB, bC = 3 * sl, 3 * sl + 1, 3 * sl + 2
            for (bank, lo, hi, off) in ((bA, 0, 512, 0), (bB, 640, 1152, 0), (bC, 512, 640, 0), (bC, 1152, 1280, 128)):
                for k in range(8):
                    S.pe(lambda e, k=k, bank=bank, lo=lo, hi=hi, off=off: e.matmul(
                        ps[bank][:, off:off + hi - lo], self.hT[:, k, c0:c0 + 128], wq[:, k, lo:hi], start=(k == 0), stop=(k == 7)),
                        r=[self.hT_t[n], wq_t], w=[pst[bank]])

        def tile(n):
            sl = n % 2
            lat = n < NLT
            bA, bB, bC = 3 * sl, 3 * sl + 1, 3 * sl + 2
            krb, kr_t = kr[sl]
            sqb, sq_t = sq[sl]
            s8, s8_t = ss8[sl]
            nqb, nq_t = nq[sl]
            S.act(lambda e: e.activation(out=self.V[:, n, :, 0:64], in_=ps[bC][:, 0:256].rearrange("p (g d) -> p g d", g=4), func=AF.Copy),
                  r=[pst[bC], vones], w=[self.V_t[n]])
            S.act(lambda e: e.activation(out=sqb, in_=ps[bB], func=AF.Square), r=[pst[bB]], w=[sq_t])
            if lat:
                rope(S.dve, n, ps[bA], pst[bA], t1s[0], t1s[1], t2s[0], t2s[1], 0, 0, krb, kr_t)
            else:
                S.act(lambda e: e.activation(out=self.q_tok[:, n, 0:384], in_=ps[bA][:, 0:384], func=AF.Copy), r=[pst[bA]], w=[self.q_t[n]])
                S.act(lambda e: e.activation(out=krb[:, 0:128], in_=ps[bA][:, 384:512], func=AF.Copy), r=[pst[bA]], w=[kr_t])
            S.dve(lambda e: e.tensor_reduce(out=s8, in_=sqb.rearrange("p (h d) -> p h d", h=8), axis=AX.X, op=ALU.add), r=[sq_t], w=[s8_t])
            S.act(lambda e: e.activation(out=s8, in_=s8, func=AF.Ln, scale=1.0 / 64.0, bias=self.epsc[:, 0:1]), r=[s8_t, self.cst2], w=[s8_t])
            S.act(lambda e: e.activation(out=s8, in_=s8, func=AF.Exp, scale=-0.5), r=[s8_t], w=[s8_t])
            nq3 = nqb.rearrange("p (h d) -> p h d", h=8)
            S.dve(lambda e: e.tensor_tensor(out=nq3, in0=ps[bB].rearrange("p (h d) -> p h d", h=8),
                                            in1=s8.unsqueeze(2).broadcast_to([128, 8, 64]), op=ALU.mult), r=[pst[bB], s8_t], w=[nq_t])
            S.dve(lambda e: e.tensor_tensor(out=nqb, in0=nqb, in1=wqk, op=ALU.mult), r=[nq_t, wqk_t], w=[nq_t])
            if lat:
                (a, a_t), (b, b_t) = t1g[sl], t2g[sl]
                rope(S.pool, n, nqb, nq_t, a, a_t, b, b_t, 384, 128, krb, kr_t)
            else:
                S.act(lambda e: e.activation(out=self.q_tok[:, n, 384:768], in_=nqb[:, 0:384], func=AF.Copy), r=[nq_t], w=[self.q_t[n]])
                S.act(lambda e: e.activation(out=krb[:, 128:256], in_=nqb[:, 384:512], func=AF.Copy), r=[nq_t], w=[kr_t])
            pb6 = ps[6 + sl].bitcast(BF16)
            for m in range(2):
                S.pe(lambda e, m=m: e.transpose(pb6[:, m * 128:(m + 1) * 128], krb[:, m * 128:(m + 1) * 128], self.ident_b),
                     r=[kr_t, self.cst], w=[pst[6 + sl]])
            S.act(lambda e: e.activation(out=self.kT[:, :, n * 128:(n + 1) * 128], in_=pb6[:, 0:256].rearrange("p (g t) -> p g t", g=2), func=AF.Copy),
                  r=[pst[6 + sl]], w=[self.kT_t[n]])

        proj(0)
        for n in range(NT):
            if n + 1 < NT:
                proj(n + 1)
            tile(n)
        self.dump("q_l%d" % l, self.q_tok.rearrange("p n f -> p (n f)"), [128, NT * 768], r=self.q_t, dt=BF16)
        self.dump("kT_l%d" % l, self.kT.rearrange("p g t -> p (g t)"), [128, 2 * NT * 128], r=self.kT_t, dt=BF16)
        self.dump("V_l%d" % l, self.V.rearrange("p n g d -> p (n g d)"), [128, NT * 4 * 65], r=self.V_t, dt=BF16)

    def gbc_build(self, l, wh, postw_dram, reg, pw, pw_t):
        S = self.S
        ps, pst = self.ps, self.pst
        Gbc = reg.alloc([128, 2, D], BF16)
        G_t = Tok("Gbc")
        S.dma("sp", pw, postw_dram[l, :].partition_broadcast(128), w=[pw_t])
        for lc in range(2):
            for half in range(2):
                bank = 2 * lc + half
                S.pe(lambda e, lc=lc, half=half, bank=bank: e.matmul(ps[bank], self.sel[:, lc, :], self.grow[:, wh, half * 512:(half + 1) * 512],
                                                                     start=True, stop=True), r=[self.grow_t, self.cst2], w=[pst[bank]])
                S.dve(lambda e, lc=lc, half=half, bank=bank: e.tensor_tensor(out=Gbc[:, lc, half * 512:(half + 1) * 512], in0=ps[bank],
                                                                             in1=pw[:, half * 512:(half + 1) * 512], op=ALU.mult),
                      r=[pst[bank], pw_t], w=[G_t])
        return Gbc, G_t

    def residual_epilogue(self, n, banks, Gbc, G_t, tmp, tmp_t, st, st_t):
        S = self.S
        ps, pst = self.ps, self.pst
        lc = 0 if n < NLT else 1
        for half, bank in enumerate(banks):
            S.act(lambda e, half=half, bank=bank: e.activation(out=tmp[:, half * 512:(half + 1) * 512], in_=ps[bank], func=AF.Square, scale=1.0 / 32.0,
                                                              accum_out=st[:, half:half + 1]), r=[pst[bank]], w=[tmp_t, st_t])
        S.dve(lambda e: e.tensor_tensor(out=st[:, 2:3], in0=st[:, 0:1], in1=st[:, 1:2], op=ALU.add), r=[st_t], w=[st_t])
        S.act(lambda e: e.activation(out=st[:, 3:4], in_=st[:, 2:3], func=AF.Ln, bias=self.epsc[:, 0:1]), r=[st_t, self.cst2], w=[st_t])
        S.act(lambda e: e.activation(out=st[:, 3:4], in_=st[:, 3:4], func=AF.Exp, scale=-0.5), r=[st_t], w=[st_t])
        for half, bank in enumerate(banks):
            S.dve(lambda e, half=half, bank=bank: e.scalar_tensor_tensor(
                out=tmp[:, half * 512:(half + 1) * 512], in0=ps[bank], scalar=st[:, 3:4], in1=Gbc[:, lc, half * 512:(half + 1) * 512],
                op0=ALU.mult, op1=ALU.mult), r=[pst[bank], st_t, G_t], w=[tmp_t])
        S.dve(lambda e: e.tensor_tensor(out=self.X[:, n, :], in0=self.X[:, n, :], in1=tmp, op=ALU.add), r=[tmp_t, self.xt[n]], w=[self.xt[n]])

    def attn_phase(self, l, need_ctx):
        S = self.S
        R = self.R0
        ps, pst = self.ps, self.pst
        wo = R.alloc([128, 8, D], BF16)
        wo_t = Tok("wo")
        for k in range(8):
            S.dma("pool", wo[:, k, :], self.w_out[l, k * 128:(k + 1) * 128, :], w=[wo_t], waw=False)
        mbp = R.alloc([128, 384], BF16)
        mbn = R.alloc([128, 384], BF16)
        mb_t = Tok("mb")
        S.dma("pool", mbp, self.mbp_in[:, :], w=[mb_t], waw=False)
        S.dma("pool", mbn, self.mbn_in[:, :], w=[mb_t], waw=False)
        esink = R.alloc([128, 6], F32)
        es_t = Tok("esink")
        S.dma("sp", esink, self.swa_sink[l, :].partition_broadcast(128), w=[es_t])
        S.act(lambda e: e.activation(out=esink, in_=esink, func=AF.Exp), r=[es_t], w=[es_t])
        tmp, tmp_t = self.nb(R, [128, D], F32, 1, "etmp")[0]
        Gbc, G_t = self.gbc_build(l, 0, self.attn_post, R, tmp, tmp_t)
        for j in range(NFC // 2):
            for half in range(2):
                src = self.w_up[l, :, half * FF + j * 256: half * FF + (j + 1) * 256].rearrange("(k p) f -> p k f", p=128)
                S.dma("pool", self.wup_bf[j, :, :, half, :], src, w=[self.wupbf_t[j]], waw=(half == 0))
        for c in range(NFC):
            S.dma("pool", self.wd_bf[:, c, :], self.w_down[l, c * 128:(c + 1) * 128, :], w=[self.wdbf_t], waw=(c == 0))
        Ra = Region(self.arena, self.R2.base, 20480)
        qT = self.nb(Ra, [128, 6, 128], BF16, 2, "qT")
        PT = self.nb(Ra, [128, 2, 384], BF16, 4, "PT")
        mix = self.nb(Ra, [128, 768], BF16, 2, "mix")
        mixT = self.nb(Ra, [128, 8, 128], BF16, 2, "mixT")
        den = self.nb(Ra, [128, 12], F32, 2, "den")
        stt = self.nb(Ra, [128, 4], F32, 2, "est")
        tiles = list(range(NT)) if need_ctx else list(range(NLT))
        LA = 2
        SB = (2, 4)
        ctr = {"p": 0, "s": 0}
        fifo = []

        def drain(limit):
            while fifo and (sum(1 for k, _ in fifo if k == "pv") > limit or fifo[0][0] == "tail"):
                kind, fn = fifo.pop(0)
                fn()

        for qi, n in enumerate(tiles):
            sl = qi % 2
            lat = n < NLT
            qTb, qT_t = qT[sl]
            dn, dn_t = den[sl]
            mx, mx_t = mix[sl]
            for m in range(2):
                pb = ps[m].bitcast(BF16)
                for j in range(3):
                    S.pe(lambda e, m=m, j=j, n=n, pb=pb: e.transpose(pb[:, j * 128:(j + 1) * 128],
                                                                     self.q_tok[:, n, m * 384 + (2 + j) * 64: m * 384 + (4 + j) * 64], self.ident_b),
                         r=[self.q_t[n], self.cst], w=[pst[m]])
                for j in range(3):
                    S.pe(lambda e, m=m, j=j, n=n, pb=pb: e.transpose(pb[0:64, j * 128:(j + 1) * 128],
                                                                     self.q_tok[:, n, m * 384 + j * 64: m * 384 + (j + 1) * 64], self.ident_b),
                         r=[self.q_t[n], self.cst], w=[pst[m]])
                S.dve(lambda e, m=m, qTb=qTb, pb=pb: e.tensor_copy(out=qTb[:, m * 3:(m + 1) * 3, :],
                                                                   in_=pb[:, 0:384].rearrange("p (h t) -> p h t", h=3)),
                      r=[pst[m]], w=[qT_t])
            for mixer in range(2):
                obank = 6 + mixer
                if mixer == 0:
                    if lat:
                        kts = ([(n - 1, mbp)] if n >= 1 else []) + [(n, None)] + ([(n + 1, mbn)] if n <= NLT - 2 else []) + [(16, None), (17, None)]
                    else:
                        kts = [(16, None), (17, None)]
                else:
                    kts = [(k, None) for k in range(NT)] if lat else [(16, None), (17, None)]
                nk = len(kts)
                for ki, (kt, mb) in enumerate(kts):
                    sb0 = SB[ctr["s"] % 2]
                    ctr["s"] += 1
                    P, P_t = PT[ctr["p"] % len(PT)]
                    ctr["p"] += 1
                    for g in range(2):
                        S.pe(lambda e, kt=kt, qTb=qTb, mixer=mixer, g=g, sb0=sb0, mb=mb: e.matmul(
                            ps[sb0 + g][:, 0:384], self.kT[g * 64:(g + 1) * 64, mixer, kt * 128:(kt + 1) * 128],
                            qTb[g * 64:(g + 1) * 64, mixer * 3:(mixer + 1) * 3, :], start=True, stop=(mb is None)),
                            r=[self.kT_t[kt], qT_t], w=[pst[sb0 + g]])
                    if mb is not None:
                        for g in range(2):
                            S.pe(lambda e, sb0=sb0, g=g, mb=mb: e.matmul(ps[sb0 + g][:, 0:384], self.ident_b, mb, start=False, stop=True),
                                 r=[self.cst, mb_t], w=[pst[sb0 + g]])
                    S.act(lambda e, P=P, sb0=sb0: e.activation(out=P, in_=self.psall[:, sb0:sb0 + 2, 0:384], func=AF.Exp, scale=0.125),
                          r=[pst[sb0], pst[sb0 + 1]], w=[P_t])
                    last = (ki == nk - 1)

                    def pv(P=P, P_t=P_t, kt=kt, ki=ki, obank=obank, nk=nk, last=last, mixer=mixer, dn=dn, dn_t=dn_t, mx=mx, mx_t=mx_t):
                        for g in range(2):
                            kvi = 2 * mixer + g
                            for h in range(3):
                                hh = 3 * g + h
                                S.pe(lambda e, h=h, hh=hh, g=g, kvi=kvi: e.matmul(
                                    ps[obank][:, hh * 65:(hh + 1) * 65], P[:, g, h * 128:(h + 1) * 128], self.V[:, kt, kvi, :],
                                    start=(ki == 0 and g == 0 and h == 0), stop=(ki == nk - 1)), r=[P_t, self.V_t[kt]], w=[pst[obank]])
                        if not last:
                            return
                        o6 = ps[obank][:, 0:390].rearrange("p (h d) -> p h d", h=6)
                        dsl = dn[:, mixer * 6:(mixer + 1) * 6]
                        if mixer == 0:
                            S.dve(lambda e: e.tensor_tensor(out=dsl.unsqueeze(2), in0=o6[:, :, 64:65], in1=esink.unsqueeze(2), op=ALU.add),
                                  r=[pst[obank], es_t], w=[dn_t])
                        else:
                            S.dve(lambda e: e.tensor_copy(out=dsl.unsqueeze(2), in_=o6[:, :, 64:65]), r=[pst[obank]], w=[dn_t])
                        S.dve(lambda e: e.reciprocal(out=dsl, in_=dsl), r=[dn_t], w=[dn_t])
                        S.dve(lambda e: e.tensor_tensor(
                            out=mx[:, mixer * 384:(mixer + 1) * 384].rearrange("p (h d) -> p h d", h=6), in0=o6[:, :, 0:64],
                            in1=dsl.unsqueeze(2).broadcast_to([128, 6, 64]), op=ALU.mult), r=[pst[obank], dn_t], w=[mx_t])

                    fifo.append(("pv", pv))
                    drain(LA)

            def tail(n=n, sl=sl, mx=mx, mx_t=mx_t):
                self.dump("mix_l%d_%d" % (l, n), mx, [128, 768], r=[mx_t], dt=BF16)
                mT, mT_t = mixT[sl]
                pb = ps[0].bitcast(BF16)
                for c in range(8):
                    src = self.oloc[:, n, c * 128:(c + 1) * 128] if c < 2 else mx[:, (c - 2) * 128:(c - 1) * 128]
                    S.pe(lambda e, c=c, src=src: e.transpose(pb[:, c * 128:(c + 1) * 128], src, self.ident_b),
                         r=[self.oloc_t[n], mx_t, self.cst], w=[pst[0]])
                S.dve(lambda e: e.tensor_copy(out=mT, in_=pb.rearrange("p (c t) -> p c t", c=8)), r=[pst[0]], w=[mT_t])
                for half in range(2):
                    for c in range(8):
                        S.pe(lambda e, c=c, half=half: e.matmul(ps[half], mT[:, c, :], wo[:, c, half * 512:(half + 1) * 512],
                                                                start=(c == 0), stop=(c == 7)), r=[mT_t, wo_t], w=[pst[half]])
                st, st_t = stt[sl]
                self.residual_epilogue(n, (0, 1), Gbc, G_t, tmp, tmp_t, st, st_t)

            fifo.append(("tail", tail))
        drain(-1)
        while fifo:
            fifo.pop(0)[1]()
        if "xattn_l%d" % l in self.debug:
            self.dump("xattn_l%d" % l, self.X.rearrange("p n f -> p (n f)"), [128, NT * D], r=self.xt)

    def ffn_phase(self, l, need_ctx):
        S = self.S
        R = self.RALL
        ps, pst = self.ps, self.pst
        wd = R.alloc([128, NFC, D], BF16)
        wd_t = Tok("wd")
        S.dma("sp", wd, self.wd_bf, r=[self.wdbf_t], w=[wd_t])
        tmp, tmp_t = self.nb(R, [128, D], F32, 1, "ftmp")[0]
        Gbc, G_t = self.gbc_build(l, 1, self.ffn_post, R, tmp, tmp_t)
        WMAX = 4 * 128 + 2
        hTbs = self.nb(R, [128, 8, WMAX], BF16, 2, "hTb")
        actT = R.alloc([128, NFC, WMAX], BF16)
        actT_t = Tok("actT")
        halo, halo_t = self.nb(R, [128, 8, 1], BF16, 1, "halo")[0]
        wu = self.nb(R, [128, 8, 2, 256], BF16, 2, "wu")
        acc = [self.nb(R, [128, 2, 256], F32, 2, "acc%d" % h) for h in range(2)]
        sil = self.nb(R, [128, 2, 256], BF16, 2, "sil")
        stt = self.nb(R, [128, 4], F32, 2, "fst")
        tiles_all = list(range(NT)) if need_ctx else list(range(NLT))
        xn = self.nb(R, [128, D], BF16, 2, "xn")
        junk, jt = xn[0]
        for j in tiles_all:
            S.act(lambda e, j=j: e.activation(out=junk, in_=self.X[:, j, :], func=AF.Square, scale=1.0 / 32.0,
                                              accum_out=self.ms[:, j:j + 1]), r=[self.xt[j]], w=[jt, self.ms_t])
        S.act(lambda e: e.activation(out=self.rstd, in_=self.ms, func=AF.Ln, bias=self.epsc[:, 0:1]), r=[self.ms_t, self.cst2], w=[self.ms_t])
        S.act(lambda e: e.activation(out=self.rstd, in_=self.rstd, func=AF.Exp, scale=-0.5), r=[self.ms_t], w=[self.ms_t])
        self._ffn_xn = xn
        self._ffn_hv1 = self.nb(R, [128, 8, 1], F32, 1, "hv1")[0]
        blocks = [[0, 1, 2, 3], [4, 5, 6, 7], [8, 9, 10, 11], [12, 13, 14, 15]] + ([[16, 17]] if need_ctx else [])
        wu_i = 0
        ep_i = 0

        def block_norm(bi):
            blk = blocks[bi]
            hTb, hTb_t = hTbs[bi % 2]
            W = len(blk) * 128 + 2
            items = []
            first, last = blk[0], blk[-1]
            zero_cols = []
            if first in (0, 16):
                zero_cols.append(0)
            else:
                S.dve(lambda e: e.tensor_copy(out=hTb[:, :, 0:1], in_=halo), r=[halo_t], w=[hTb_t])
            for i, n in enumerate(blk):
                items.append((n, 0, 128, 1 + i * 128))
            if last in (15, 17):
                zero_cols.append(W - 1)
            else:
                items.append((last + 1, 0, 1, W - 1))
            for zc in zero_cols:
                S.dve(lambda e, zc=zc: e.memset(hTb[:, :, zc:zc + 1], 0.0), w=[hTb_t])
            self.norm_tiles_ffn(l, items, hTb, hTb_t)
            S.dve(lambda e, W=W: e.tensor_copy(out=halo, in_=hTb[:, :, W - 2:W - 1]), r=[hTb_t], w=[halo_t])

        block_norm(0)
        self.stop("f1")
        for bi, blk in enumerate(blocks):
            hTb, hTb_t = hTbs[bi % 2]
            W = len(blk) * 128 + 2
            Lg = (W - 2) // 2
            groups = ((0, Lg + 2), (Lg, W))
            for j in range(NFC // 2):
                wub, wu_t = wu[wu_i % 2]
                wu_i += 1
                S.dma("sp", wub, self.wup_bf[j], r=[self.wupbf_t[j]], w=[wu_t])
                for pi in range(2):
                    i = 2 * j + pi
                    par = i % 2
                    for half in range(2):
                        for gi, (c_lo, c_hi) in enumerate(groups):
                            bank = 4 * par + 2 * half + gi
                            for k in range(8):
                                S.pe(lambda e, k=k, bank=bank, wub=wub, half=half, pi=pi, c_lo=c_lo, c_hi=c_hi, hTb=hTb: e.matmul(
                                    ps[bank][:, 0:c_hi - c_lo], wub[:, k, half, pi * 128:(pi + 1) * 128], hTb[:, k, c_lo:c_hi],
                                    start=(k == 0), stop=(k == 7)), r=[wu_t, hTb_t], w=[pst[bank]])
                    self._ffn_pair_post(l, i, par, Lg, acc, sil, actT, actT_t)
                    continue
                    (aa, aa_t), (ag, ag_t) = acc[0][par], acc[1][par]
                    sb_, sb_t = sil[par]
                    pv = [self.psall[:, 4 * par + 2 * half:4 * par + 2 * half + 2, :] for half in range(2)]
                    ptk = [[pst[4 * par + 2 * half], pst[4 * par + 2 * half + 1]] for half in range(2)]
                    ccs = [self.convcol[:, l, half * NFC + i, :] for half in range(2)]
                    for half, (a, a_t) in enumerate(((aa, aa_t), (ag, ag_t))):
                        S.act(lambda e, a=a, half=half: e.activation(out=a[:, :, 0:Lg], in_=pv[half][:, :, 1:1 + Lg], func=AF.Identity,
                                                                    scale=ccs[half][:, 1:2], bias=ccs[half][:, 3:4]), r=ptk[half] + [self.cst], w=[a_t])
                    for half, (a, a_t) in enumerate(((aa, aa_t), (ag, ag_t))):
                        for (off, wi) in ((0, 0), (2, 2)):
                            S.dve(lambda e, a=a, half=half, off=off, wi=wi: e.scalar_tensor_tensor(
                                out=a[:, :, 0:Lg], in0=pv[half][:, :, off:off + Lg], scalar=ccs[half][:, wi:wi + 1], in1=a[:, :, 0:Lg],
                                op0=ALU.mult, op1=ALU.add), r=ptk[half] + [self.cst, a_t], w=[a_t])
                        if half == 0:
                            S.act(lambda e: e.activation(out=sb_[:, :, 0:Lg], in_=aa[:, :, 0:Lg], func=AF.Silu), r=[aa_t], w=[sb_t])
                    S.dve(lambda e, i=i: e.tensor_tensor(out=actT[:, i, 1:1 + 2 * Lg].rearrange("p (g t) -> p g t", g=2),
                                                         in0=sb_[:, :, 0:Lg], in1=ag[:, :, 0:Lg], op=ALU.mult),
                          r=[sb_t, ag_t], w=[actT_t])
                    if bi == 0 and i == 0: self.stop("f2")
            if bi == 0: self.stop("f3")
            if bi + 1 < len(blocks):
                block_norm(bi + 1)
            if bi == 0: self.stop("f4")
            for i, n in enumerate(blk):
                banks = (2 * (ep_i % 2), 2 * (ep_i % 2) + 1)
                st, st_t = stt[ep_i % 2]
                ep_i += 1
                for half in range(2):
                    for c in range(NFC):
                        S.pe(lambda e, c=c, half=half, i=i, bank=banks[half]: e.matmul(
                            self.ps[bank], actT[:, c, 1 + i * 128:1 + (i + 1) * 128], wd[:, c, half * 512:(half + 1) * 512],
                            start=(c == 0), stop=(c == NFC - 1)), r=[actT_t, wd_t], w=[pst[banks[half]]])
                self.residual_epilogue(n, banks, Gbc, G_t, tmp, tmp_t, st, st_t)
        if "xout_l%d" % l in self.debug:
            self.dump("xout_l%d" % l, self.X.rearrange("p n f -> p (n f)"), [128, NT * D], r=self.xt)

    def _ffn_pair_post(self, l, i, par, Lg, acc, sil, actT, actT_t):
        S = self.S
        pst = self.pst
        (aa, aa_t), (ag, ag_t) = acc[0][par], acc[1][par]
        sb_, sb_t = sil[par]
        pv = [self.psall[:, 4 * par + 2 * half:4 * par + 2 * half + 2, :] for half in range(2)]
        ptk = [[pst[4 * par + 2 * half], pst[4 * par + 2 * half + 1]] for half in range(2)]
        ccs = [self.convcol[:, l, half * NFC + i, :] for half in range(2)]
        for half, (a, a_t) in enumerate(((aa, aa_t), (ag, ag_t))):
            S.act(lambda e, a=a, half=half: e.activation(out=a[:, :, 0:Lg], in_=pv[half][:, :, 1:1 + Lg], func=AF.Identity,
                                                        scale=ccs[half][:, 1:2], bias=ccs[half][:, 3:4]), r=ptk[half] + [self.cst], w=[a_t])
        for half, (a, a_t) in enumerate(((aa, aa_t), (ag, ag_t))):
            for (off, wi) in ((0, 0), (2, 2)):
                S.dve(lambda e, a=a, half=half, off=off, wi=wi: e.scalar_tensor_tensor(
                    out=a[:, :, 0:Lg], in0=pv[half][:, :, off:off + Lg], scalar=ccs[half][:, wi:wi + 1], in1=a[:, :, 0:Lg],
                    op0=ALU.mult, op1=ALU.add), r=ptk[half] + [self.cst, a_t], w=[a_t])
            if half == 0:
                S.act(lambda e: e.activation(out=sb_[:, :, 0:Lg], in_=aa[:, :, 0:Lg], func=AF.Silu), r=[aa_t], w=[sb_t])
        S.dve(lambda e: e.tensor_tensor(out=actT[:, i, 1:1 + 2 * Lg].rearrange("p (g t) -> p g t", g=2),
                                        in0=sb_[:, :, 0:Lg], in1=ag[:, :, 0:Lg], op=ALU.mult),
              r=[sb_t, ag_t], w=[actT_t])

    def norm_tiles_ffn(self, l, items, dest, dtok):
        S = self.S
        xn = self._ffn_xn
        for i, (n, lo, hi, dc) in enumerate(items):
            sl = i % 2
            lc = 0 if n < NLT else 1
            xnb, xnt = xn[sl]
            bank, bt = self.ps[2 + sl], self.pst[2 + sl]
            S.dve(lambda e, n=n, xnb=xnb: e.tensor_scalar_mul(out=xnb, in0=self.X[:, n, :], scalar1=self.rstd[:, n:n + 1]),
                  r=[self.xt[n], self.ms_t], w=[xnt])
            pb = bank.bitcast(BF16)
            for c in range(8):
                S.pe(lambda e, c=c, xnb=xnb, pb=pb: e.transpose(pb[:, c * 128:(c + 1) * 128], xnb[:, c * 128:(c + 1) * 128], self.ident_b),
                     r=[xnt, self.cst], w=[bt])
            w = hi - lo
            if w == 1:
                hv, hvt = self._ffn_hv1
                Ab = self.ABcol[:, 1, 0, :, lc:lc + 1]
                Bb = self.ABcol[:, 1, 1, :, lc:lc + 1]
                pv = pb.rearrange("p (c t) -> p c t", c=8)[:, :, lo:hi]
                S.dve(lambda e, pv=pv, hv=hv, Ab=Ab: e.tensor_tensor(out=hv, in0=pv, in1=Ab, op=ALU.mult), r=[bt, self.ABcol_t], w=[hvt])
                S.dve(lambda e, hv=hv, Bb=Bb, dc=dc: e.tensor_tensor(out=dest[:, :, dc:dc + 1], in0=hv, in1=Bb, op=ALU.add),
                      r=[hvt, self.ABcol_t], w=[dtok])
                continue
            for c in range(8):
                S.act(lambda e, c=c, pb=pb, lo=lo, hi=hi, dc=dc, w=w, lc=lc: e.activation(
                    out=dest[:, c, dc:dc + w], in_=pb[:, c * 128 + lo:c * 128 + hi], func=AF.Identity,
                    scale=self.ABcol[:, 1, 0, c, lc:lc + 1], bias=self.ABcol[:, 1, 1, c, lc:lc + 1]), r=[bt, self.ABcol_t], w=[dtok], waw=False)


def make_in_maps(inputs, n_cores=8):
    f = lambda a: np.ascontiguousarray(np.asarray(a, dtype=np.float32))
    x, c, ctx, c_ctx = f(inputs["x"]), f(inputs["c"]), f(inputs["ctx"]), f(inputs["c_ctx"])
    L = DEPTH
    normcol = np.stack([f(inputs["attn_pre_norm"]), f(inputs["ffn_pre_norm"])], axis=1)
    normcol = normcol.reshape(L, 2, 8, 128).transpose(3, 0, 1, 2).copy()
    cw = f(inputs["ffn_conv_w"])
    cb = f(inputs["ffn_conv_b"])
    conv = np.concatenate([cw, cb[:, None, :]], axis=1)
    convcol = conv.reshape(L, 4, 44, 128).transpose(3, 0, 2, 1).copy()
    wg = f(inputs["gla_w_gate"])
    bg = f(inputs["gla_b_gate"])
    wg_aug = np.zeros((L, 33, 256), np.float32)
    wg_aug[:, 0:16, 0:128] = wg[:, 0]
    wg_aug[:, 16:32, 128:256] = wg[:, 1]
    wg_aug[:, 32, 0:128] = bg[:, 0]
    wg_aug[:, 32, 128:256] = bg[:, 1]
    qn, kn = f(inputs["gqa_q_norm"]), f(inputs["gqa_k_norm"])
    gqa_qkn = np.concatenate([np.tile(qn, (1, 6)), np.tile(kn, (1, 2))], axis=1).copy()
    cos_t, sin_t = _rope_tables()
    s_idx = np.arange(128)[:, None]
    t_idx = np.arange(128)[None, :]
    NEG = np.float32(-30000.0)
    mb_prev = np.where(s_idx >= t_idx, np.float32(0), NEG).astype(np.float32)
    mb_next = np.where(s_idx <= t_idx, np.float32(0), NEG).astype(np.float32)
    shared = {
        "w_mod": f(inputs["w_mod"]), "b_mod": f(inputs["b_mod"]), "normcol": normcol,
        "attn_post_norm": f(inputs["attn_post_norm"]), "ffn_post_norm": f(inputs["ffn_post_norm"]),
        "w_in": f(inputs["w_in"]), "wg_aug": wg_aug, "gla_out_norm": f(inputs["gla_out_norm"]),
        "swa_sink": f(inputs["swa_sink"]), "gqa_qkn": gqa_qkn,
        "w_out": f(inputs["w_out"]), "ffn_w_up": f(inputs["ffn_w_up"]), "convcol": convcol,
        "ffn_w_down": f(inputs["ffn_w_down"]),
        "c_ident": np.eye(128, dtype=np.float32), "c_triu": np.triu(np.ones((128, 128), np.float32)),
        "c_tril": np.tril(np.ones((128, 128), np.float32)),
        "c_u16": (-np.triu(np.ones((128, 128), np.float32)) / 16.0).astype(np.float32),
        "c_l16": (-np.tril(np.ones((128, 128), np.float32)) / 16.0).astype(np.float32),
        "c_bm": (np.arange(128)[:, None] // 32 == np.arange(256)[None, :] // 64).astype(np.float32),
        "c_bmq": (np.arange(128)[:, None] // 32 == np.arange(512)[None, :] // 128).astype(np.float32),
        "c_mb_prev": np.tile(mb_prev, (1, 3)).copy(), "c_mb_next": np.tile(mb_next, (1, 3)).copy(),
        "c_cos": cos_t, "c_sin": sin_t,
    }
    maps = []
    for b in range(n_cores):
        ccol = np.stack([c[b].reshape(8, 128).T, c_ctx.reshape(8, 128).T], axis=-1).copy()
        m = dict(shared)
        m.update({"x": x[b], "ctx": ctx[b], "ccol": ccol})
        maps.append(m)
    return maps


def kernel(**inputs):
    b = Builder()
    nc = b.build()
    maps = make_in_maps(inputs)
    res = run_bass_kernel_spmd(nc, maps, core_ids=list(range(8)))
    return np.stack([np.asarray(r["y"], dtype=np.float32) for r in res.results], axis=0)
```
